# Optimizing a Trainium2 kernel written in Bass

```python
import math
import jax, jax.numpy as jnp
from jax import lax
import numpy as np

D_MODEL = 2048
BATCH = 2
SEQ = 16384
DEPTH = 2
DEC_BATCH = 32
DEC_SEQ = 64
PAST_LEN = 2048

CHUNK = 64
D_MIX = D_MODEL
SSM_WIDTH = D_MIX // 4
ATTN_WIDTH = D_MIX // 2
CONV_WIDTH = D_MIX - SSM_WIDTH - ATTN_WIDTH
SSM_GROUP = 16
SSM_GROUPS = SSM_WIDTH // SSM_GROUP
SSM_STATE = 64
HEAD_DIM = 64
N_HEADS = ATTN_WIDTH // HEAD_DIM
N_KV_HEADS = 2
Q_PER_KV = N_HEADS // N_KV_HEADS
KV_WIDTH = N_KV_HEADS * HEAD_DIM
WINDOW = 128
WINDOW_CHUNKS = WINDOW // CHUNK
BAND = (WINDOW_CHUNKS + 1) * CHUNK
CONV_K = 3
D_FF = 4 * D_MODEL
REL_BUCKETS = 32
REL_MAX_DIST = 64
EPS = 1e-6
NEG_INF = -1e30
IN_COLS = SSM_WIDTH + ATTN_WIDTH + 2 * KV_WIDTH + 3 * CONV_WIDTH

kernel_name = 'hybrid_streaming_encoder_step'


def rmsnorm(x, gain=None):
    xf = x.astype(jnp.float32)
    y = xf * lax.rsqrt(jnp.mean(xf * xf, axis=-1, keepdims=True) + EPS)
    if gain is not None:
        y = y * gain.astype(jnp.float32)
    return y.astype(x.dtype)


def rel_bias_block(table, n_q, n_k, key_offset):
    qi = jnp.arange(n_q)[:, None]
    ks = jnp.arange(n_k)[None, :]
    rel = ks - key_offset - qi
    half = REL_BUCKETS // 2
    exact = half // 2
    n = jnp.abs(rel)
    nf = jnp.maximum(n, 1).astype(jnp.float32)
    far = exact + (jnp.log(nf / exact) / math.log(REL_MAX_DIST / exact) * (half - exact)).astype(jnp.int32)
    far = jnp.minimum(far, half - 1)
    bucket = jnp.where(rel > 0, half, 0) + jnp.where(n < exact, n, far)
    bias = table[bucket].astype(jnp.float32)
    return jnp.transpose(bias, (2, 0, 1)).reshape(N_KV_HEADS, Q_PER_KV, n_q, n_k)


def sink_attention(q, k, v, bias, sinks, key_mask):
    logits = jnp.einsum('bnqkgd,bnskd->bnkgqs', q, k, preferred_element_type=jnp.float32)
    logits = logits * (HEAD_DIM ** -0.5) + bias
    if key_mask is not None:
        logits = jnp.where(key_mask[None, :, None, None, None, :], logits, NEG_INF)
    sink = sinks.astype(jnp.float32).reshape(1, 1, N_KV_HEADS, Q_PER_KV, 1, 1)
    m = jnp.maximum(jnp.max(logits, axis=-1, keepdims=True), sink)
    p = jnp.exp(logits - m)
    probs = p / (jnp.sum(p, axis=-1, keepdims=True) + jnp.exp(sink - m))
    return jnp.einsum('bnkgqs,bnskd->bnqkgd', probs.astype(v.dtype), v)


def band_attention_prompt(q, k, v, bias, sinks):
    b, L = q.shape[0], q.shape[1]
    nc = L // CHUNK
    qc = q.reshape(b, nc, CHUNK, N_KV_HEADS, Q_PER_KV, HEAD_DIM)

    def band(t):
        tp = jnp.pad(t, ((0, 0), (WINDOW, 0), (0, 0), (0, 0)))
        tp = tp.reshape(b, nc + WINDOW_CHUNKS, CHUNK, N_KV_HEADS, HEAD_DIM)
        return jnp.concatenate([tp[:, j:j + nc] for j in range(WINDOW_CHUNKS + 1)], axis=2)

    key_pos = jnp.arange(nc)[:, None] * CHUNK - WINDOW + jnp.arange(BAND)[None, :]
    out = sink_attention(qc, band(k), band(v), bias, sinks, key_pos >= 0)
    return out.reshape(b, L, ATTN_WIDTH)


def _complex_affine_combine(e1, e2):
    a1r, a1i, b1r, b1i = e1
    a2r, a2i, b2r, b2i = e2
    ar = a1r * a2r - a1i * a2i
    ai = a1r * a2i + a1i * a2r
    br = a2r * b1r - a2i * b1i + b2r
    bi = a2r * b1i + a2i * b1r + b2i
    return ar, ai, br, bi


def ssm_mixer(u, a_re, a_im, log_dt, b_re, b_im, c_re, c_im, d_skip, w_glu, h0_re, h0_im):
    f32 = jnp.float32
    b, L = u.shape[0], u.shape[1]
    uf = u.astype(f32).reshape(b, L, SSM_GROUPS, SSM_GROUP)
    ar, ai = a_re.astype(f32), a_im.astype(f32)
    dt = jnp.exp(log_dt.astype(f32))[:, None]
    mag = jnp.exp(dt * ar)
    abar_re, abar_im = mag * jnp.cos(dt * ai), mag * jnp.sin(dt * ai)
    den = ar * ar + ai * ai
    f_re = ((abar_re - 1.0) * ar + abar_im * ai) / den
    f_im = (abar_im * ar - (abar_re - 1.0) * ai) / den
    bu_re = jnp.einsum('blgh,gph->blgp', uf, b_re.astype(f32))
    bu_im = jnp.einsum('blgh,gph->blgp', uf, b_im.astype(f32))
    bb_re = f_re * bu_re - f_im * bu_im
    bb_im = f_re * bu_im + f_im * bu_re
    shape = (1, L, SSM_GROUPS, SSM_STATE)
    ac_re, ac_im, h_re, h_im = lax.associative_scan(
        _complex_affine_combine,
        (jnp.broadcast_to(abar_re, shape), jnp.broadcast_to(abar_im, shape), bb_re, bb_im),
        axis=1)
    if h0_re is not None:
        g_re = h0_re.astype(f32)[:, None]
        g_im = h0_im.astype(f32)[:, None]
        h_re, h_im = (h_re + ac_re * g_re - ac_im * g_im,
                      h_im + ac_re * g_im + ac_im * g_re)
    y = (jnp.einsum('blgp,ghp->blgh', h_re, c_re.astype(f32))
         - jnp.einsum('blgp,ghp->blgh', h_im, c_im.astype(f32))
         + d_skip.astype(f32).reshape(SSM_GROUPS, SSM_GROUP) * uf)
    y = jax.nn.gelu(y.reshape(b, L, SSM_WIDTH))
    y = y * jax.nn.sigmoid(y @ w_glu.astype(f32))
    return y.astype(u.dtype), h_re[:, -1], h_im[:, -1]


def causal_conv(zp, w, L):
    return sum(zp[:, j:j + L] * w[:, j] for j in range(CONV_K))


def trunk_layer(x, mod, bias, w_in, a_re, a_im, log_dt, b_re, b_im, c_re, c_im, d_skip, w_glu,
                q_g, k_g, sinks, conv_w, out_g, w_out, w_ff1, w_ff2,
                kv_k=None, kv_v=None, h0_re=None, h0_im=None, conv_buf=None):
    b, L = x.shape[0], x.shape[1]
    shift1, scale1, gate1, shift2, scale2, gate2 = jnp.split(mod[:, None, :], 6, axis=-1)
    h = rmsnorm(x) * (1.0 + scale1) + shift1
    proj = h @ w_in
    sizes = [SSM_WIDTH, ATTN_WIDTH, KV_WIDTH, KV_WIDTH, CONV_WIDTH, CONV_WIDTH, CONV_WIDTH]
    u, q, k, v, gb, gc, xc = jnp.split(proj, np.cumsum(sizes)[:-1].tolist(), axis=-1)

    y_ssm, new_re, new_im = ssm_mixer(u, a_re, a_im, log_dt, b_re, b_im, c_re, c_im, d_skip, w_glu, h0_re, h0_im)

    q = rmsnorm(q.reshape(b, L, N_HEADS, HEAD_DIM), q_g)
    k = rmsnorm(k.reshape(b, L, N_KV_HEADS, HEAD_DIM), k_g)
    v = v.reshape(b, L, N_KV_HEADS, HEAD_DIM)
    if kv_k is None:
        y_attn = band_attention_prompt(q, k, v, bias, sinks)
        new_k, new_v = k[:, -WINDOW:], v[:, -WINDOW:]
    else:
        n_buf = kv_k.shape[1]
        kf = jnp.concatenate([kv_k.astype(k.dtype), k], axis=1)
        vf = jnp.concatenate([kv_v.astype(v.dtype), v], axis=1)
        qg = q.reshape(b, 1, L, N_KV_HEADS, Q_PER_KV, HEAD_DIM)
        y_attn = sink_attention(qg, kf[:, None], vf[:, None], bias, sinks, None).reshape(b, L, ATTN_WIDTH)
        new_k, new_v = kf[:, -n_buf:], vf[:, -n_buf:]

    z = gc * xc
    if conv_buf is None:
        conv_buf = jnp.zeros((b, CONV_K - 1, CONV_WIDTH), z.dtype)
    zp = jnp.concatenate([conv_buf.astype(z.dtype), z], axis=1)
    y_conv = gb * causal_conv(zp, conv_w, L)
    new_conv = zp[:, -(CONV_K - 1):]

    y = jnp.concatenate([rmsnorm(y_ssm), rmsnorm(y_attn), rmsnorm(y_conv)], axis=-1) * out_g
    x = x + gate1 * (y @ w_out)

    h2 = rmsnorm(x) * (1.0 + scale2) + shift2
    x = x + gate2 * (jnp.square(jax.nn.relu(h2 @ w_ff1)) @ w_ff2)
    return x, new_k, new_v, new_re, new_im, new_conv


def setup_inputs(seed: int = 0) -> dict:
    key = jax.random.key(seed)
    ks = jax.random.split(key, 32)
    f32 = jnp.float32

    def nrm(k, shape, scale):
        return scale * jax.random.normal(k, shape, f32)

    n_buf = min(WINDOW, PAST_LEN)
    p_idx = jnp.arange(SSM_STATE, dtype=f32)
    return {
        'x_prompt': nrm(ks[0], (BATCH, SEQ, D_MODEL), 1.0),
        'x_sample': nrm(ks[1], (DEC_BATCH, DEC_SEQ, D_MODEL), 1.0),
        'cache_k': nrm(ks[2], (DEPTH, DEC_BATCH, n_buf, N_KV_HEADS, HEAD_DIM), 1.0),
        'cache_v': nrm(ks[3], (DEPTH, DEC_BATCH, n_buf, N_KV_HEADS, HEAD_DIM), 1.0),
        'state_ssm_re': nrm(ks[4], (DEPTH, DEC_BATCH, SSM_GROUPS, SSM_STATE), 0.3),
        'state_ssm_im': nrm(ks[5], (DEPTH, DEC_BATCH, SSM_GROUPS, SSM_STATE), 0.3),
        'state_conv': nrm(ks[6], (DEPTH, DEC_BATCH, CONV_K - 1, CONV_WIDTH), 1.0),
        'c_prompt': nrm(ks[7], (BATCH, D_MODEL), 1.0),
        'c_sample': nrm(ks[8], (DEC_BATCH, D_MODEL), 1.0),
        'rel_bias': nrm(ks[9], (REL_BUCKETS, N_HEADS), 0.5),
        'w_ada': nrm(ks[10], (DEPTH, D_MODEL, 6 * D_MODEL), 0.3 * D_MODEL ** -0.5),
        'b_ada': nrm(ks[11], (DEPTH, 6 * D_MODEL), 0.02),
        'w_in': nrm(ks[12], (DEPTH, D_MODEL, IN_COLS), D_MODEL ** -0.5),
        'ssm_a_re': -0.5 + nrm(ks[13], (DEPTH, SSM_GROUPS, SSM_STATE), 0.01),
        'ssm_a_im': math.pi * p_idx + nrm(ks[14], (DEPTH, SSM_GROUPS, SSM_STATE), 0.01),
        'ssm_log_dt': jax.random.uniform(ks[15], (DEPTH, SSM_GROUPS), f32, math.log(0.001), math.log(0.1)),
        'ssm_b_re': nrm(ks[16], (DEPTH, SSM_GROUPS, SSM_STATE, SSM_GROUP), (2 * SSM_GROUP) ** -0.5),
        'ssm_b_im': nrm(ks[17], (DEPTH, SSM_GROUPS, SSM_STATE, SSM_GROUP), (2 * SSM_GROUP) ** -0.5),
        'ssm_c_re': nrm(ks[18], (DEPTH, SSM_GROUPS, SSM_GROUP, SSM_STATE), (2 * SSM_STATE) ** -0.5),
        'ssm_c_im': nrm(ks[19], (DEPTH, SSM_GROUPS, SSM_GROUP, SSM_STATE), (2 * SSM_STATE) ** -0.5),
        'ssm_d': nrm(ks[20], (DEPTH, SSM_WIDTH), 1.0),
        'ssm_w_glu': nrm(ks[21], (DEPTH, SSM_WIDTH, SSM_WIDTH), SSM_WIDTH ** -0.5),
        'q_norm_g': 1.0 + nrm(ks[22], (DEPTH, HEAD_DIM), 0.05),
        'k_norm_g': 1.0 + nrm(ks[23], (DEPTH, HEAD_DIM), 0.05),
        'attn_sinks': nrm(ks[24], (DEPTH, N_HEADS), 1.0),
        'conv_w': nrm(ks[25], (DEPTH, CONV_WIDTH, CONV_K), CONV_K ** -0.5),
        'out_norm_g': 1.0 + nrm(ks[26], (DEPTH, D_MIX), 0.05),
        'w_out': nrm(ks[27], (DEPTH, D_MIX, D_MODEL), D_MIX ** -0.5),
        'w_ff1': nrm(ks[28], (DEPTH, D_MODEL, D_FF), D_MODEL ** -0.5),
        'w_ff2': nrm(ks[29], (DEPTH, D_FF, D_MODEL), D_FF ** -0.5),
    }


def reference(x_prompt, x_sample, cache_k, cache_v, state_ssm_re, state_ssm_im, state_conv,
              c_prompt, c_sample, rel_bias, w_ada, b_ada, w_in, ssm_a_re, ssm_a_im, ssm_log_dt,
              ssm_b_re, ssm_b_im, ssm_c_re, ssm_c_im, ssm_d, ssm_w_glu, q_norm_g, k_norm_g,
              attn_sinks, conv_w, out_norm_g, w_out, w_ff1, w_ff2):
    n_buf = cache_k.shape[2]
    s_new = x_sample.shape[1]
    bias_prompt = rel_bias_block(rel_bias, CHUNK, BAND, WINDOW)
    bias_sample = rel_bias_block(rel_bias, s_new, n_buf + s_new, n_buf)
    xp, xs = x_prompt, x_sample
    kp_l, vp_l, rp_l, ip_l, cp_l = [], [], [], [], []
    ks_l, vs_l, rs_l, is_l, cs_l = [], [], [], [], []
    for l in range(DEPTH):
        lw = (w_in[l], ssm_a_re[l], ssm_a_im[l], ssm_log_dt[l], ssm_b_re[l], ssm_b_im[l],
              ssm_c_re[l], ssm_c_im[l], ssm_d[l], ssm_w_glu[l], q_norm_g[l], k_norm_g[l],
              attn_sinks[l], conv_w[l], out_norm_g[l], w_out[l], w_ff1[l], w_ff2[l])
        mod_p = jax.nn.silu(c_prompt) @ w_ada[l] + b_ada[l]
        mod_s = jax.nn.silu(c_sample) @ w_ada[l] + b_ada[l]
        xp, kp, vp, rp, ip, cp = trunk_layer(xp, mod_p, bias_prompt, *lw)
        xs, ks, vs, rs, is_, cs = trunk_layer(xs, mod_s, bias_sample, *lw,
                                              cache_k[l], cache_v[l], state_ssm_re[l],
                                              state_ssm_im[l], state_conv[l])
        kp_l.append(kp); vp_l.append(vp); rp_l.append(rp); ip_l.append(ip); cp_l.append(cp)
        ks_l.append(ks); vs_l.append(vs); rs_l.append(rs); is_l.append(is_); cs_l.append(cs)
    return (xp, xs,
            jnp.stack(kp_l), jnp.stack(vp_l), jnp.stack(rp_l), jnp.stack(ip_l), jnp.stack(cp_l),
            jnp.stack(ks_l), jnp.stack(vs_l), jnp.stack(rs_l), jnp.stack(is_l), jnp.stack(cs_l))
```

```python
import math
import numpy as np
from contextlib import ExitStack
import concourse.bass as bass
import concourse.mybir as mybir
from concourse.bass_utils import run_bass_kernel_spmd

F32 = mybir.dt.float32
BF16 = mybir.dt.bfloat16
I32 = mybir.dt.int32
AF = mybir.ActivationFunctionType
ALU = mybir.AluOpType
ENGS = ['pe', 'act', 'dve', 'pool', 'sp']
EPS = 1e-6
NT = 256
NPT = 16
LC = 64
TWO_PI = 2.0 * math.pi


class Op:
    __slots__ = ('eng', 'fn', 'deps', 'sig', 'dma', 'slot', 'use', 'cnt')

    def __init__(self, eng, fn, dma):
        self.eng = eng
        self.fn = fn
        self.deps = set()
        self.sig = False
        self.dma = dma
        self.slot = -1
        self.use = 0
        self.cnt = 0


class Sched:
    def __init__(self, n_dma_sems=40):
        self.ops = {e: [] for e in ENGS}
        self.res = {}
        self.n_dma = 0
        self.P = n_dma_sems
        self.slot_last = [None] * n_dma_sems
        self.qpools = {'sp': (0, 16), 'pool': (16, 3), 'act': (19, 4)}
        self.qn = {}

    def add(self, eng, fn, reads=(), writes=(), dma=False):
        op = Op(eng, fn, dma)
        deps = op.deps
        psr = [k for k in reads if isinstance(k, tuple) and k[0] == 'ps']
        if psr:
            reads = [k for k in reads if k not in psr]
            writes = list(writes) + [k for k in psr if k not in writes]
        for k in reads:
            r = self.res.get(k)
            if r is not None and r[0] is not None:
                deps.add(r[0])
        for k in writes:
            r = self.res.get(k)
            if r is not None:
                if r[0] is not None:
                    deps.add(r[0])
                deps.update(r[1])
        for k in reads:
            r = self.res.get(k)
            if r is None:
                r = [None, []]
                self.res[k] = r
            if not dma:
                r[1] = [o for o in r[1] if o.dma or o.eng != eng]
            r[1].append(op)
        for k in writes:
            self.res[k] = [op, []]
        if dma:
            base, cnt = self.qpools[eng]
            n = self.qn.get(eng, 0)
            s = base + n % cnt
            op.slot = s
            op.use = n // cnt + 1
            if self.slot_last[s] is not None:
                deps.add(self.slot_last[s])
            self.slot_last[s] = op
            self.qn[eng] = n + 1
            self.n_dma += 1
        deps.discard(op)
        self.ops[eng].append(op)
        return op

    EPOCH = 12000

    def emit(self, block, sems, dma_sems):
        for e in ENGS:
            for op in self.ops[e]:
                for d in op.deps:
                    if d.dma:
                        continue
                    if d.eng == 'pe' and op.eng == 'pe' and not op.dma:
                        continue
                    d.sig = True
        for e in ENGS:
            c = 0
            for op in self.ops[e]:
                if op.sig and not op.dma:
                    c += 1
                op.cnt = c
            assert c <= self.EPOCH * len(sems[e]), (e, c)

        EP = self.EPOCH

        def run(e, eng):
            waited = {}
            for op in self.ops[e]:
                for d in op.deps:
                    if d.dma:
                        key = ('d', d.slot)
                        sem = dma_sems[d.slot]
                        val = 16 * d.use
                    else:
                        if d.eng == e and e == 'pe' and not op.dma:
                            continue
                        ep = (d.cnt - 1) // EP
                        key = (d.eng, ep)
                        sem = sems[d.eng][ep]
                        val = d.cnt - ep * EP
                    if waited.get(key, 0) >= val:
                        continue
                    eng.wait_ge(sem, val)
                    waited[key] = val
                if op.fn is None:
                    continue
                inst = op.fn(eng)
                if op.dma:
                    inst.then_inc(dma_sems[op.slot], 16)
                elif op.sig:
                    inst.then_inc(sems[e][(op.cnt - 1) // EP], 1)

        @block.tensor
        def _(eng):
            run('pe', eng)

        @block.scalar
        def _(eng):
            run('act', eng)

        @block.vector
        def _(eng):
            run('dve', eng)

        @block.gpsimd
        def _(eng):
            run('pool', eng)

        @block.sync
        def _(eng):
            run('sp', eng)


MATS = {'in': (13, 16), 'out': (8, 16), 'ff1': (32, 16), 'ff2': (32, 16), 'ada': (48, 16)}


class _Stop(Exception):
    pass


def build_program(kstop=None):
    nc = bass.Bass("TRN2", target_bir_lowering=False)
    S = Sched()
    stage = [0]

    def checkpoint():
        stage[0] += 1
        if kstop is not None and stage[0] >= kstop:
            raise _Stop()

    def din(name, shape, dt=F32):
        return nc.dram_tensor(name, list(shape), dt, kind="ExternalInput").ap()

    def dout(name, shape, dt=F32):
        return nc.dram_tensor(name, list(shape), dt, kind="ExternalOutput").ap()

    xp = din("xp", [NPT * NT, 2048])
    xs = din("xs", [256, 2048])
    ck = din("ck", [2, 4, 128, 128])
    cv = din("cv", [2, 4, 128, 128])
    sre = din("sre", [2, 4, 2048])
    sim = din("sim", [2, 4, 2048])
    scv = din("scv", [2, 4, 2, 512])
    cvec = din("cvec", [5, 2048])
    flags = din("flags", [1, 16])
    ident_d = din("ident", [128, 128])
    iota_d = din("iota", [1, LC])
    ohe_d = din("ohe", [32, 256])
    rel_bias = din("rel_bias", [32, 16])
    w_ada = din("w_ada", [2, 2048, 12288])
    b_ada = din("b_ada", [2, 12288])
    w_in = din("w_in", [2, 2048, 3328])
    a_re_d = din("ssm_a_re", [2, 2048])
    a_im_d = din("ssm_a_im", [2, 2048])
    log_dt_d = din("ssm_log_dt", [2, 32])
    b_re_d = din("ssm_b_re", [2, 32, 64, 16])
    b_im_d = din("ssm_b_im", [2, 32, 64, 16])
    c_re_d = din("ssm_c_re", [2, 32, 16, 64])
    c_im_d = din("ssm_c_im", [2, 32, 16, 64])
    ssm_d_d = din("ssm_d", [2, 512])
    w_glu_d = din("ssm_w_glu", [2, 512, 512])
    q_g_d = din("q_norm_g", [2, 64])
    k_g_d = din("k_norm_g", [2, 64])
    sinks_d = din("attn_sinks", [2, 16])
    conv_w_d = din("conv_w", [2, 512, 3])
    out_g_d = din("out_norm_g", [2, 2048])
    w_out = din("w_out", [2, 2048, 2048])
    w_ff1 = din("w_ff1", [2, 2048, 8192])
    w_ff2 = din("w_ff2", [2, 8192, 2048])

    yp = dout("yp", [NPT * NT, 2048])
    ys = dout("ys", [256, 2048])
    nkp = dout("nkp", [2, 128, 128])
    nvp = dout("nvp", [2, 128, 128])
    srp = dout("srp", [2, 2048])
    sip = dout("sip", [2, 2048])
    cvp = dout("cvp", [2, 2, 512])
    nks = dout("nks", [2, 4, 128, 128])
    nvs = dout("nvs", [2, 4, 128, 128])
    srs = dout("srs", [2, 4, 2048])
    sis = dout("sis", [2, 4, 2048])
    cvs = dout("cvs", [2, 4, 2, 512])
    out_keys = []

    wsc = {}
    for m, (nch, _) in MATS.items():
        for l in range(2):
            wsc[(m, l)] = nc.dram_tensor("wsc_%s%d" % (m, l), [nch, 128, 4096], BF16)
    x1s = nc.dram_tensor("x1s", [NPT + 1, 128, 16 * NT], F32)
    tvd = nc.dram_tensor("tvd", [16, 256], F32)
    EXW = 432
    cc_in = [nc.dram_tensor("cc_in%d" % l, [128, EXW], F32) for l in range(2)]
    cc_out = [nc.dram_tensor("cc_out%d" % l, [512, EXW], F32) for l in range(2)]

    with ExitStack() as es:
        def sb(name, shape, dt=F32):
            return es.enter_context(nc.sbuf_tensor(name + "_sb", list(shape), dt))

        xT = sb("xT", [128, 16, NT])
        hb = sb("hb", [128, 16, NT], BF16)
        ygrp = sb("ygrp", [128, 8, NT])
        yT = sb("yT", [128, 16, NT], BF16)
        qT = sb("qT", [128, 8, NT], BF16)
        uT = sb("uT", [128, 4, NT])
        uTb = sb("uTb", [128, 4, NT], BF16)
        gbT = sb("gbT", [128, 4, NT])
        ZW = 4 * (2 + 64)
        zT = sb("zT", [128, 4, 2 + NT + 8])
        knf = sb("knf", [128, NT])
        knb = sb("knb", [128, NT], BF16)
        kdup = sb("kdup", [128, 2, 12 * 64], BF16)
        vtok = sb("vtok", [64, 12, 2, 128], BF16)
        vf = sb("vf", [64, 4, 128])
        hid = sb("hid", [128, 16, NT], BF16)
        wring = sb("wring", [128, 3, 4096], BF16)
        xstage = sb("xstage", [128, 2048])
        ident = sb("identsb", [128, 128])
        identb = sb("identb", [128, 128], BF16)
        onesb = sb("onesb", [128, 128], BF16)
        ones64 = sb("ones64", [128, 128], BF16)
        selk = sb("selk", [128, 2, 128], BF16)
        flg = sb("flg", [128, 16])
        iota = sb("iota", [128, LC])
        sq = sb("sq", [128, 2, NT], BF16)
        rstd = sb("rstd", [128, 3, NT])
        tmpf = sb("tmpf", [128, 4, NT])
        modT = sb("modT", [128, 2, 96, 5])
        sc1p = sb("sc1p", [128, 2, 2, 16, 5])
        scT = sb("scT", [128, 16, 5], BF16)
        cT = sb("cT", [128, 16, 5])
        badaT = sb("badaT", [128, 2, 96])
        prm = sb("prm", [128, 2, 64])
        PQG, PKG, POG, PSD, PCW = 0, 1, 2, 18, 22
        sst = sb("sst", [128, 24, 16])
        tabE = sb("tabE", [128, 2, 16, LC])
        tabW = sb("tabW", [128, 2, 16, LC])
        fBw = sb("fBw", [128, 16, 2, 128], BF16)
        Cw = sb("Cw", [128, 16, 2, 128], BF16)
        bcst = sb("bcst", [128, 2, 128])
        wglu = sb("wglu", [128, 4, 512], BF16)
        biasT = sb("biasT", [64, 3, 2, 512])
        es16 = sb("es16", [128, 16])
        tvs = sb("tvs", [16, 256])
        tperm = sb("tperm", [32, 16])
        ohe = sb("ohe", [32, 256])
        PT = sb("PT", [64, 2, 3, 512], BF16)
        tS = sb("tS", [64, 2, 512])
        car = sb("car", [128, 6, 2, 16])
        ssmt = sb("ssmt", [128, 8, NT])
        hrb = sb("hrb", [128, 2, NT], BF16)
        exb = sb("exb", [128, EXW])
        hal = sb("hal", [128, EXW])
        a4k = sb("a4k", [128, 6, 16])
        fence = sb("fence", [128, 4])
        o64 = sb("o64", [64, 2, 128])
        tang = ssmt[:, 0:4, :].rearrange("p a b -> p (a b)")
        tang2 = ssmt[:, 4:8, :].rearrange("p a b -> p (a b)")
        tangi = xstage[:, 1024:2048].bitcast(I32)
        atmp = xstage[:, :].rearrange("p (a b) -> p a b", a=4)
        gth = xstage[:, 0:4 * EXW].rearrange("p (r c) -> p r c", r=4)
        TANG = [('ssmt', i) for i in range(4)]
        TANG2 = [('ssmt', i) for i in range(4, 8)]

        banks = [es.enter_context(nc.psum_tensor("bank%d" % i, [128, 512], F32)) for i in range(8)]
        NEP = {"pe": 3, "act": 2, "dve": 8, "pool": 1, "sp": 1}
        sems = {e: [es.enter_context(nc.semaphore("s_%s%d" % (e, i))) for i in range(NEP[e])] for e in ENGS}
        dsem = [es.enter_context(nc.semaphore("d%d" % i)) for i in range(S.P)]
        block = es.enter_context(nc.Block())

        rot = {'acc': [0, [0, 1, 2]], 'aux': [0, [3, 4]], 'att': [0, [5, 6, 7]]}

        def bank(kind):
            r = rot[kind]
            b = r[1][r[0] % len(r[1])]
            r[0] += 1
            return b

        def PB(b):
            return ('ps', b)

        def mm(out, lhsT, rhs, start, stop, reads, writes):
            S.add('pe', lambda e: e.matmul(out, lhsT=lhsT, rhs=rhs, start=start, stop=stop), reads, writes)

        def tr(out, in_, idn, reads, writes):
            S.add('pe', lambda e: e.transpose(out=out, in_=in_, identity=idn), reads, writes)

        def act(out, in_, func, reads, writes, scale=1.0, bias=0.0):
            S.add('act', lambda e: e.activation(out=out, in_=in_, func=func, scale=scale, bias=bias), reads, writes)

        def tt(out, a, b, op, reads, writes, eng='dve'):
            S.add(eng, lambda e: e.tensor_tensor(out=out, in0=a, in1=b, op=op), reads, writes)

        def ts(out, a, s1, s2, op0, op1, reads, writes, eng='dve'):
            S.add(eng, lambda e: e.tensor_scalar(out=out, in0=a, scalar1=s1, scalar2=s2, op0=op0, op1=op1), reads, writes)

        def stt(out, a, sc, b, op0, op1, reads, writes):
            S.add('dve', lambda e: e.scalar_tensor_tensor(out=out, in0=a, scalar=sc, in1=b, op0=op0, op1=op1), reads, writes)

        def cp(out, in_, reads, writes, eng='dve'):
            S.add(eng, lambda e: e.tensor_copy(out=out, in_=in_), reads, writes)

        def rcp(out, in_, reads, writes):
            S.add('dve', lambda e: e.reciprocal(out=out, in_=in_), reads, writes)

        def scan(out, d0, d1, init, reads, writes):
            S.add('dve', lambda e: e.tensor_tensor_scan(out=out, data0=d0, data1=d1, initial=init,
                                                        op0=ALU.mult, op1=ALU.add), reads, writes)

        def mset(ap, val, reads, writes, eng='dve'):
            S.add(eng, lambda e: e.memset(ap, val), reads, writes)

        def dma(out, in_, reads, writes, eng='sp', slow=False):
            if slow:
                S.add(eng, lambda e: e.dma_start(out=out, in_=in_, allow_slow_non_contiguous=True), reads, writes, dma=True)
            else:
                S.add(eng, lambda e: e.dma_start(out=out, in_=in_), reads, writes, dma=True)

        def src_view(m, l, ci):
            if m == 'in':
                c0 = ci * 256
                return w_in[l].rearrange("(kt p) c -> p kt c", p=128)[:, :, c0:c0 + 256]
            if m == 'out':
                return w_out[l].rearrange("(kt p) c -> p kt c", p=128)[:, :, ci * 256:(ci + 1) * 256]
            if m == 'ff1':
                return w_ff1[l].rearrange("(kt p) c -> p kt c", p=128)[:, :, ci * 256:(ci + 1) * 256]
            if m == 'ada':
                return w_ada[l].rearrange("(kt p) c -> p kt c", p=128)[:, :, ci * 256:(ci + 1) * 256]
            if m == 'ff2':
                gi, oc = ci // 8, ci % 8
                return w_ff2[l][gi * 2048:(gi + 1) * 2048, :].rearrange("(kt p) c -> p kt c", p=128)[:, :, oc * 256:(oc + 1) * 256]

        def cast_all(m, l):
            for ci in range(MATS[m][0]):
                dst = wsc[(m, l)][ci].rearrange("p (kt c) -> p kt c", kt=16)
                dma(dst, src_view(m, l, ci), [], [('wsc', m, l, ci)], eng='pool')

        plan = []
        plan_pos = [0, 0]

        def w_topup():
            while plan_pos[1] < len(plan) and plan_pos[1] < plan_pos[0] + 3:
                n = plan_pos[1]
                m, l, ci = plan[n]
                slot = n % 3
                dma(wring[:, slot, :], wsc[(m, l)][ci], [('wsc', m, l, ci)], [('w', slot)])
                plan_pos[1] += 1

        def w_get(m, l, ci):
            n = plan_pos[0]
            assert plan[n] == (m, l, ci), (plan[n], (m, l, ci))
            w_topup()
            plan_pos[0] += 1
            slot = n % 3
            return wring[:, slot, :].rearrange("p (kt c) -> p kt c", kt=16), ('w', slot)

        dma(ident[:], ident_d[:, :], [], ['ident'])
        dma(flg[:], flags[0:1, :].partition_broadcast(128), [], ['flg'])
        dma(iota[:], iota_d[0:1, :].partition_broadcast(128), [], ['iota'])
        dma(ohe[:], ohe_d[:, :], [], ['ohe'])
        for kv_ in range(2):
            for a_ in range(2):
                d0 = kv_ * 8 + a_ * 4
                s0 = kv_ * 8 + a_
                dma(tperm[:, d0:d0 + 4], rel_bias[:, kv_ * 8:kv_ * 8 + 8].rearrange("b (t a) -> b t a", a=2)[:, :, a_], [], ['tperm'], slow=True)
        cp(identb[:], ident[:], ['ident'], ['identb'])
        mset(onesb[:], 1.0, [], ['onesb'])
        mset(ones64[:], 0.0, [], ['ones64'])
        mset(ones64[0:64, 0:64], 1.0, ['ones64'], ['ones64'])
        mset(ones64[64:128, 64:128], 1.0, ['ones64'], ['ones64'])
        mset(selk[:], 0.0, [], ['selk'])
        for kv in range(2):
            for hf in range(2):
                cp(selk[kv * 64:(kv + 1) * 64, kv, hf * 64:(hf + 1) * 64], ident[kv * 64:(kv + 1) * 64, kv * 64:(kv + 1) * 64],
                   ['ident', 'selk'], ['selk'])
        mset(fence[:], 0.0, [], ['fence'])
        for r_ in range(5):
            dma(cT[:, :, r_], cvec[r_].rearrange("(kt p) -> p kt", p=128), [], ['cT'], slow=True)
        act(tmpf[:, 0, 0:80], cT[:].rearrange("p a b -> p (a b)"), AF.Sigmoid, ['cT'], ['tmpf0'])
        tt(scT[:].rearrange("p a b -> p (a b)"), tmpf[:, 0, 0:80], cT[:].rearrange("p a b -> p (a b)"), ALU.mult,
           ['tmpf0', 'cT'], ['scT'])
        for l in range(2):
            dma(badaT[:, l, :], b_ada[l].rearrange("(ft p) -> p ft", p=128), [], [('bada', l)], slow=True)

        b0 = bank('aux')
        mm(banks[b0][0:16, 0:256], tperm[:, :], ohe[:, :], True, True, ['tperm', 'ohe'], [PB(b0)])
        cp(tvs[:], banks[b0][0:16, 0:256], [PB(b0)], ['tvs'])
        dma(tvd[:, :], tvs[:], ['tvs'], ['tvd'])
        for kap in range(64):
            for j_ in range(3):
                src = bass.AP(tensor=tvd, offset=63 - kap + 64 * j_, ap=[[0, 1], [256, 16], [1, 64]])
                dma(biasT[kap:kap + 1, j_, :, :].rearrange("p k (h q) -> p (k h) q", q=64), src, ['tvd'], ['biasT'])

        def load_layer_params(l):
            P = ('prm', l)
            for hf in range(2):
                dma(prm[hf * 64:(hf + 1) * 64, l, PQG:PQG + 1], q_g_d[l].rearrange("(p o) -> p o", o=1), [], [P], slow=True)
                dma(prm[hf * 64:(hf + 1) * 64, l, PKG:PKG + 1], k_g_d[l].rearrange("(p o) -> p o", o=1), [], [P], slow=True)
            ts(prm[:, l, PQG:PQG + 1], prm[:, l, PQG:PQG + 1], 0.125, None, ALU.mult, ALU.bypass, [P], [P])
            dma(prm[:, l, POG:POG + 16], out_g_d[l].rearrange("(ft p) -> p ft", p=128), [], [P], slow=True)
            dma(prm[:, l, PSD:PSD + 4], ssm_d_d[l].rearrange("(ft p) -> p ft", p=128), [], [P], slow=True)
            dma(prm[:, l, PCW:PCW + 12].rearrange("p (ft k) -> p ft k", k=3), conv_w_d[l].rearrange("(ft p) k -> p ft k", p=128),
                [], [P], slow=True)
            dma(es16[:], sinks_d[l:l + 1, :].partition_broadcast(128), [P], ['es16'])

        def sincos(out_sin, out_cos, phi, W, rk, wk):
            for which, dst in ((0, out_sin), (1, out_cos)):
                y = tang[:, 0:W]
                ts(y, phi, 1.0 / TWO_PI, 0.5 + 0.25 * which, ALU.mult, ALU.add, rk + TANG, TANG)
                cp(tangi[:, 0:W], y, TANG + ['xstage'], ['xstage'])
                cp(tang2[:, 0:W], tangi[:, 0:W], ['xstage'] + TANG2, TANG2)
                tt(y, y, tang2[:, 0:W], ALU.subtract, TANG + TANG2, TANG)
                ts(tang2[:, 0:W], y, 0.0, None, ALU.is_lt, ALU.bypass, TANG + TANG2, TANG2)
                tt(y, y, tang2[:, 0:W], ALU.add, TANG + TANG2, TANG)
                ts(y, y, 1.0, -0.5, ALU.min, ALU.add, TANG, TANG)
                act(dst, y, AF.Sin, TANG, wk, scale=TWO_PI)

        def load_ssm(l):
            K = ('sst', l)
            dma(sst[:, 0, :], a_re_d[l].rearrange("(s q) -> q s", q=128), [], [K], slow=True)
            dma(sst[:, 1, :], a_im_d[l].rearrange("(s q) -> q s", q=128), [], [K], slow=True)
            for gh in range(2):
                src = log_dt_d[l].rearrange("(s g) -> g s", g=2)[gh:gh + 1, :].partition_broadcast(64)
                dma(sst[gh * 64:(gh + 1) * 64, 2, :], src, [], [K], slow=True)
            act(sst[:, 2, :], sst[:, 2, :], AF.Exp, [K], [K])
            tt(sst[:, 9, :], sst[:, 2, :], sst[:, 0, :], ALU.mult, [K], [K])
            act(sst[:, 3, :], sst[:, 9, :], AF.Exp, [K], [K])
            tt(sst[:, 4, :], sst[:, 2, :], sst[:, 1, :], ALU.mult, [K], [K])
            sincos(sst[:, 6, :], sst[:, 5, :], sst[:, 4, :], 16, [K], [K])
            tt(sst[:, 10, :], sst[:, 3, :], sst[:, 5, :], ALU.mult, [K], [K])
            tt(sst[:, 11, :], sst[:, 3, :], sst[:, 6, :], ALU.mult, [K], [K])
            ts(sst[:, 12, :], sst[:, 10, :], -1.0, None, ALU.add, ALU.bypass, [K], [K])
            tt(sst[:, 13, :], sst[:, 0, :], sst[:, 0, :], ALU.mult, [K], [K])
            tt(sst[:, 14, :], sst[:, 1, :], sst[:, 1, :], ALU.mult, [K], [K])
            tt(sst[:, 13, :], sst[:, 13, :], sst[:, 14, :], ALU.add, [K], [K])
            rcp(sst[:, 13, :], sst[:, 13, :], [K], [K])
            tt(sst[:, 14, :], sst[:, 12, :], sst[:, 0, :], ALU.mult, [K], [K])
            tt(sst[:, 15, :], sst[:, 11, :], sst[:, 1, :], ALU.mult, [K], [K])
            tt(sst[:, 14, :], sst[:, 14, :], sst[:, 15, :], ALU.add, [K], [K])
            tt(sst[:, 7, :], sst[:, 14, :], sst[:, 13, :], ALU.mult, [K], [K])
            tt(sst[:, 14, :], sst[:, 11, :], sst[:, 0, :], ALU.mult, [K], [K])
            tt(sst[:, 15, :], sst[:, 12, :], sst[:, 1, :], ALU.mult, [K], [K])
            tt(sst[:, 14, :], sst[:, 14, :], sst[:, 15, :], ALU.subtract, [K], [K])
            tt(sst[:, 8, :], sst[:, 14, :], sst[:, 13, :], ALU.mult, [K], [K])
            ang = atmp[:, 0:2, :].rearrange("p a b -> p (a b)")
            tt(ang.rearrange("p (s t) -> p s t", t=LC), sst[:, 4, :].unsqueeze(2).to_broadcast([128, 16, LC]),
               iota[:].unsqueeze(1).to_broadcast([128, 16, LC]), ALU.mult, [K, 'iota', 'xstage'], ['xstage'])
            TE = ('tab', l)
            sincos(tabE[:, 1].rearrange("p s t -> p (s t)"), tabE[:, 0].rearrange("p s t -> p (s t)"), ang, 16 * LC,
                   ['xstage'], [TE])
            frb = sst[:, 7, :].unsqueeze(2).to_broadcast([128, 16, LC])
            fib = sst[:, 8, :].unsqueeze(2).to_broadcast([128, 16, LC])
            t0 = atmp[:, 0:2, :].rearrange("p a (s t) -> p (a s) t", t=LC)
            t1 = atmp[:, 2:4, :].rearrange("p a (s t) -> p (a s) t", t=LC)
            tt(t0, tabE[:, 0], frb, ALU.mult, [TE, K, 'xstage'], ['xstage'])
            tt(t1, tabE[:, 1], fib, ALU.mult, [TE, K, 'xstage'], ['xstage'])
            tt(tabW[:, 0], t0, t1, ALU.add, ['xstage'], [TE])
            tt(t0, tabE[:, 0], fib, ALU.mult, [TE, K, 'xstage'], ['xstage'])
            tt(t1, tabE[:, 1], frb, ALU.mult, [TE, K, 'xstage'], ['xstage'])
            tt(tabW[:, 1], t0, t1, ALU.subtract, ['xstage'], [TE])
            for m in range(3):
                n = 4096.0 * (m + 1)
                act(a4k[:, 2 * m, :], sst[:, 9, :], AF.Exp, [K], [('a4k', l)], scale=n)
                ts(sst[:, 16, :], sst[:, 4, :], n, None, ALU.mult, ALU.bypass, [K], [K])
                sincos(sst[:, 17, :], sst[:, 18, :], sst[:, 16, :], 16, [K], [K])
                tt(a4k[:, 2 * m + 1, :], a4k[:, 2 * m, :], sst[:, 17, :], ALU.mult, [K, ('a4k', l)], [('a4k', l)])
                tt(a4k[:, 2 * m, :], a4k[:, 2 * m, :], sst[:, 18, :], ALU.mult, [K, ('a4k', l)], [('a4k', l)])
            for (wt, dre, dim_, isB) in ((fBw, b_re_d, b_im_d, True), (Cw, c_re_d, c_im_d, False)):
                WK = ('bcw', l, isB)
                for s in range(16):
                    mset(bcst[:], 0.0, [WK, 'bcst'], ['bcst'], eng='pool')
                    for c, dd in ((0, dre), (1, dim_)):
                        for gh in range(2):
                            g = 2 * s + gh
                            r0 = 32 * (s % 4) + 16 * gh
                            if isB:
                                dma(bcst[r0:r0 + 16, c, 64 * gh:64 * gh + 64], dd[l, g].rearrange("p h -> h p"),
                                    ['bcst'], ['bcst'], eng='pool', slow=True)
                            else:
                                dma(bcst[64 * gh:64 * gh + 64, c, r0:r0 + 16], dd[l, g].rearrange("h p -> p h"),
                                    ['bcst'], ['bcst'], eng='pool', slow=True)
                    cp(wt[:, s, :, :], bcst[:], ['bcst'], [WK], eng='pool')
            dma(wglu[:], w_glu_d[l].rearrange("(kt p) c -> p kt c", p=128), [], [('wglu', l)], eng='pool')

        def compute_mod(l):
            MK = ('mod', l)
            for ci in range(48):
                wv, wk = w_get('ada', l, ci)
                for f in range(2):
                    ftm = 2 * ci + f
                    b = bank('aux')
                    for kt in range(16):
                        mm(banks[b][:, 0:5], wv[:, kt, f * 128:(f + 1) * 128], scT[:, kt, :], kt == 0, kt == 15,
                           [wk, 'scT'], [PB(b)])
                    ts(modT[:, l, ftm, :], banks[b][:, 0:5], badaT[:, l, ftm:ftm + 1], None, ALU.add, ALU.bypass,
                       [PB(b), ('bada', l)], [MK])
            ts(sc1p[:, l, 0], modT[:, l, 16:32, :], 1.0, None, ALU.add, ALU.bypass, [MK], [MK])
            ts(sc1p[:, l, 1], modT[:, l, 64:80, :], 1.0, None, ALU.add, ALU.bypass, [MK], [MK])

        def tile_info(t):
            if t < NPT:
                return dict(t=t, segs=[(0, NT, 0, 0)], prompt=True, first=(t == 0), last=(t == NPT - 1))
            return dict(t=t, segs=[(i * 64, 64, 1 + i, 1 + i) for i in range(4)], prompt=False, first=True, last=True)

        XK = [('xT', ft) for ft in range(16)]
        HK = [('hb', kt) for kt in range(16)]

        def load_x0(ti):
            t = ti['t']
            src = xp if ti['prompt'] else xs
            r0 = t * NT if ti['prompt'] else 0
            for blk in range(NT // 128):
                dma(xstage[:], src[r0 + blk * 128:r0 + (blk + 1) * 128, :], [], ['xstage'], eng='pool')
                for f4 in range(4):
                    b = bank('aux')
                    for j in range(4):
                        ft = f4 * 4 + j
                        tr(banks[b][:, j * 128:(j + 1) * 128], xstage[:, ft * 128:(ft + 1) * 128], ident[:],
                           ['xstage', 'ident'], [PB(b)])
                    S.add('act', (lambda e, o=xT[:, f4 * 4:f4 * 4 + 4, blk * 128:(blk + 1) * 128],
                                  i=banks[b][:, :].rearrange("p (j c) -> p j c", j=4): e.activation(out=o, in_=i, func=AF.Copy)),
                          [PB(b)], [('xT', f4 * 4 + j) for j in range(4)])

        def load_x1(ti):
            dma(xT[:].rearrange("p a b -> p (a b)"), x1s[ti['t']], [('x1s', ti['t'])], XK, eng='pool')

        def store_x1(ti):
            dma(x1s[ti['t']], xT[:].rearrange("p a b -> p (a b)"), XK, [('x1s', ti['t'])], eng='pool')

        def store_y(ti):
            t = ti['t']
            dst = yp if ti['prompt'] else ys
            r0 = t * NT if ti['prompt'] else 0
            for blk in range(NT // 128):
                for f4 in range(4):
                    b = bank('aux')
                    for j in range(4):
                        ft = f4 * 4 + j
                        tr(banks[b][:, j * 128:(j + 1) * 128], xT[:, ft, blk * 128:(blk + 1) * 128], ident[:],
                           [('xT', ft), 'ident'], [PB(b)])
                    S.add('act', (lambda e, o=xstage[:, f4 * 512:(f4 + 1) * 512], i=banks[b][:, :]:
                                  e.activation(out=o, in_=i, func=AF.Copy)), [PB(b)], ['xstage'])
                dma(dst[r0 + blk * 128:r0 + (blk + 1) * 128, :], xstage[:], ['xstage'], ['yout'], eng='pool')

        def norm_mod(l, ti, which):
            MK = ('mod', l)
            b = bank('aux')
            for ft in range(16):
                act(sq[:, ft % 2, :], xT[:, ft, :], AF.Square, [('xT', ft)], [('sq', ft % 2)])
                mm(banks[b][:, 0:NT], onesb[:, :], sq[:, ft % 2, :], ft == 0, ft == 15, ['onesb', ('sq', ft % 2)], [PB(b)])
            act(rstd[:, 0, :], banks[b][:, 0:NT], AF.Sqrt, [PB(b)], ['rstd0'], scale=1.0 / 2048.0, bias=EPS)
            rcp(rstd[:, 0, :], rstd[:, 0, :], ['rstd0'], ['rstd0'])
            shift_base = 0 if which == 0 else 48
            for ft in range(16):
                for (c0, ln, r, sid) in ti['segs']:
                    tm = tmpf[:, ft % 2, c0:c0 + ln]
                    stt(tm, xT[:, ft, c0:c0 + ln], sc1p[:, l, which, ft, r:r + 1], rstd[:, 0, c0:c0 + ln], ALU.mult, ALU.mult,
                        [('xT', ft), MK, 'rstd0'], [('tmpf', ft % 2)])
                    act(hb[:, ft, c0:c0 + ln], tm, AF.Identity, [('tmpf', ft % 2), MK], [('hb', ft)],
                        bias=modT[:, l, shift_base + ft, r:r + 1])

        def head_norm(psb, gidx, l, out_bf, out_f32, rk, wk):
            act(sq[:, 0, :], banks[psb][:, 0:NT], AF.Square, [PB(psb)], [('sq', 0)])
            b2 = bank('aux')
            mm(banks[b2][:, 0:NT], ones64[:, :], sq[:, 0, :], True, True, ['ones64', ('sq', 0)], [PB(b2)])
            act(rstd[:, 1, :], banks[b2][:, 0:NT], AF.Sqrt, [PB(b2)], ['rstd1'], scale=1.0 / 64.0, bias=EPS)
            rcp(rstd[:, 1, :], rstd[:, 1, :], ['rstd1'], ['rstd1'])
            if out_f32 is not None:
                stt(out_f32, banks[psb][:, 0:NT], prm[:, l, gidx:gidx + 1], rstd[:, 1, :], ALU.mult, ALU.mult,
                    [PB(psb), 'rstd1', ('prm', l)], wk)
                cp(out_bf, out_f32, wk, rk)
            else:
                stt(out_bf, banks[psb][:, 0:NT], prm[:, l, gidx:gidx + 1], rstd[:, 1, :], ALU.mult, ALU.mult,
                    [PB(psb), 'rstd1', ('prm', l)], rk)

        def seg_zoff(ti, si):
            return si * 66 if not ti['prompt'] else 0

        def in_proj(l, ti, parts):
            order = []
            if 'u' in parts:
                order += [0, 1]
            if 'q' in parts:
                order += [2, 3, 4, 5]
            if 'kv' in parts:
                order += [6]
            if 'conv' in parts:
                order += [7, 8, 11, 12, 9, 10]
            for ci in order:
                wv, wk = w_get('in', l, ci)
                for f in range(2):
                    ft = 2 * ci + f
                    if ft == 13:
                        for ch in range(NT // 64):
                            b = bank('acc')
                            for kt in range(16):
                                mm(banks[b][0:64, 0:128], hb[:, kt, ch * 64:(ch + 1) * 64], wv[:, kt, 128:256], kt == 0, kt == 15,
                                   [('hb', kt), wk], [PB(b)])
                            cp(vf[:, ch, :], banks[b][0:64, 0:128], [PB(b)], [('vf', ch)])
                            for (c0, ln, r, sid) in ti['segs']:
                                if c0 <= ch * 64 < c0 + ln:
                                    si = ti['segs'].index((c0, ln, r, sid))
                                    kc = (si * 3 if not ti['prompt'] else 0) + 2 + (ch * 64 - c0) // 64
                            for dup in range(2):
                                act(vtok[:, kc, :, dup * 64:(dup + 1) * 64], vf[:, ch, :].rearrange("p (k d) -> p k d", k=2), AF.Copy,
                                    [('vf', ch)], [('vtok', kc)])
                        continue
                    b = bank('acc')
                    for kt in range(16):
                        mm(banks[b][:, 0:NT], wv[:, kt, f * 128:(f + 1) * 128], hb[:, kt, :], kt == 0, kt == 15,
                           [('hb', kt), wk], [PB(b)])
                    if ft < 4:
                        act(uT[:, ft, :], banks[b][:, 0:NT], AF.Copy, [PB(b)], [('uT', ft)])
                        cp(uTb[:, ft, :], banks[b][:, 0:NT], [PB(b)], [('uTb', ft)])
                    elif ft < 12:
                        head_norm(b, PQG, l, qT[:, ft - 4, :], None, [('qT', ft - 4)], None)
                    elif ft == 12:
                        head_norm(b, PKG, l, knb[:, :], knf[:, :], ['knb'], ['knf'])
                        for kv in range(2):
                            b2 = bank('aux')
                            mm(banks[b2][:, 0:NT], selk[:, kv, :], knb[:, :], True, True, ['selk', 'knb'], [PB(b2)])
                            for si, (c0, ln, r, sid) in enumerate(ti['segs']):
                                kc0 = (si * 3 if not ti['prompt'] else 0) + 2
                                act(kdup[:, kv, kc0 * 64:kc0 * 64 + ln], banks[b2][:, c0:c0 + ln], AF.Copy, [PB(b2)],
                                    [('kdup', si)])
                    elif ft < 18:
                        act(gbT[:, ft - 14, :], banks[b][:, 0:NT], AF.Copy, [PB(b)], [('gbT', ft - 14)])
                    elif ft >= 22:
                        j = ft - 22
                        for si, (c0, ln, r, sid) in enumerate(ti['segs']):
                            zo = seg_zoff(ti, si) + 2
                            act(zT[:, j, zo:zo + ln], banks[b][:, c0:c0 + ln], AF.Copy, [PB(b)], [('zT', j)])
                    else:
                        j = ft - 18
                        for si, (c0, ln, r, sid) in enumerate(ti['segs']):
                            zo = seg_zoff(ti, si) + 2
                            tt(zT[:, j, zo:zo + ln], banks[b][:, c0:c0 + ln], zT[:, j, zo:zo + ln], ALU.mult,
                               [PB(b), ('zT', j)], [('zT', j)])

        def ssm(l, ti, full, carry_of):
            K = ('sst', l)
            TE = ('tab', l)
            T = [('ssmt', i) for i in range(8)]
            for s in range(16):
                ftu = s // 4
                bre, bim = bank('att'), bank('att')
                mm(banks[bre][:, 0:NT], fBw[:, s, 0, :], uTb[:, ftu, :], True, True, [('bcw', l, True), ('uTb', ftu)], [PB(bre)])
                mm(banks[bim][:, 0:NT], fBw[:, s, 1, :], uTb[:, ftu, :], True, True, [('bcw', l, True), ('uTb', ftu)], [PB(bim)])
                nsub = NT // LC

                def v3(ap):
                    return ap.rearrange("p (a b) -> p a b", b=LC)
                Wr = tabW[:, 0, s, :].unsqueeze(1).to_broadcast([128, nsub, LC])
                Wi = tabW[:, 1, s, :].unsqueeze(1).to_broadcast([128, nsub, LC])
                tt(v3(ssmt[:, 0, :]), v3(banks[bre][:, 0:NT]), Wr, ALU.mult, [PB(bre), TE], [T[0]])
                tt(v3(ssmt[:, 1, :]), v3(banks[bim][:, 0:NT]), Wi, ALU.mult, [PB(bim), TE], [T[1]])
                tt(ssmt[:, 2, :], ssmt[:, 0, :], ssmt[:, 1, :], ALU.subtract, [T[0], T[1]], [T[2]])
                tt(v3(ssmt[:, 0, :]), v3(banks[bre][:, 0:NT]), Wi, ALU.mult, [PB(bre), TE, T[0]], [T[0]])
                tt(v3(ssmt[:, 1, :]), v3(banks[bim][:, 0:NT]), Wr, ALU.mult, [PB(bim), TE, T[1]], [T[1]])
                tt(ssmt[:, 3, :], ssmt[:, 0, :], ssmt[:, 1, :], ALU.add, [T[0], T[1]], [T[3]])
                for (c0, ln, r, sid) in ti['segs']:
                    cid = carry_of(sid)
                    CK = ('car', cid, s)
                    for sc in range(ln // LC):
                        a, bnd = c0 + sc * LC, c0 + (sc + 1) * LC
                        magb = sst[:, 3, s:s + 1].to_broadcast([128, LC])
                        scan(ssmt[:, 4, a:bnd], magb, ssmt[:, 2, a:bnd], car[:, cid, 0, s:s + 1], [K, T[2], CK], [T[4]])
                        scan(ssmt[:, 5, a:bnd], magb, ssmt[:, 3, a:bnd], car[:, cid, 1, s:s + 1], [K, T[3], CK], [T[5]])
                        if full:
                            lo, hi = 0, LC
                        else:
                            lo, hi = LC - 1, LC
                        w = hi - lo
                        Cc = tabE[:, 0, s, lo:hi]
                        Sc = tabE[:, 1, s, lo:hi]
                        gr = ssmt[:, 4, a + lo:a + hi]
                        gi = ssmt[:, 5, a + lo:a + hi]
                        tt(ssmt[:, 0, a + lo:a + hi], gr, Cc, ALU.mult, [T[4], TE, T[0]], [T[0]])
                        tt(ssmt[:, 1, a + lo:a + hi], gi, Sc, ALU.mult, [T[5], TE, T[1]], [T[1]])
                        tt(ssmt[:, 6, a + lo:a + hi], ssmt[:, 0, a + lo:a + hi], ssmt[:, 1, a + lo:a + hi], ALU.subtract,
                           [T[0], T[1]], [T[6]])
                        tt(ssmt[:, 0, a + lo:a + hi], gr, Sc, ALU.mult, [T[4], TE, T[0]], [T[0]])
                        tt(ssmt[:, 1, a + lo:a + hi], gi, Cc, ALU.mult, [T[5], TE, T[1]], [T[1]])
                        tt(ssmt[:, 7, a + lo:a + hi], ssmt[:, 0, a + lo:a + hi], ssmt[:, 1, a + lo:a + hi], ALU.add,
                           [T[0], T[1]], [T[7]])
                        cp(car[:, cid, 0, s:s + 1], ssmt[:, 6, bnd - 1:bnd], [T[6]], [CK])
                        cp(car[:, cid, 1, s:s + 1], ssmt[:, 7, bnd - 1:bnd], [T[7]], [CK])
                if full:
                    act(hrb[:, 0, :], ssmt[:, 6, :], AF.Copy, [T[6]], [('hrb', s % 2, 0)])
                    act(hrb[:, 1, :], ssmt[:, 7, :], AF.Copy, [T[7]], [('hrb', s % 2, 1)], scale=-1.0)
                    if s % 4 == 0:
                        ybank = bank('acc')
                        ssm.yb = ybank
                    yb = ssm.yb
                    mm(banks[yb][:, 0:NT], Cw[:, s, 0, :], hrb[:, 0, :], s % 4 == 0, False, [('bcw', l, False), ('hrb', s % 2, 0)], [PB(yb)])
                    mm(banks[yb][:, 0:NT], Cw[:, s, 1, :], hrb[:, 1, :], False, s % 4 == 3, [('bcw', l, False), ('hrb', s % 2, 1)], [PB(yb)])
                    if s % 4 == 3:
                        ft = s // 4
                        stt(ygrp[:, ft, :], uT[:, ft, :], prm[:, l, PSD + ft:PSD + ft + 1], banks[yb][:, 0:NT], ALU.mult, ALU.add,
                            [('uT', ft), ('prm', l), PB(yb)], [('yg', ft)])
                        act(tmpf[:, 2, :], ygrp[:, ft, :], AF.Square, [('yg', ft)], [('tmpf', 2)])
                        ts(tmpf[:, 2, :], tmpf[:, 2, :], 0.044715, 1.0, ALU.mult, ALU.add, [('tmpf', 2)], [('tmpf', 2)])
                        tt(tmpf[:, 2, :], tmpf[:, 2, :], ygrp[:, ft, :], ALU.mult, [('tmpf', 2), ('yg', ft)], [('tmpf', 2)])
                        act(tmpf[:, 2, :], tmpf[:, 2, :], AF.Sigmoid, [('tmpf', 2)], [('tmpf', 2)], scale=2.0 * math.sqrt(2.0 / math.pi))
                        tt(ygrp[:, ft, :], ygrp[:, ft, :], tmpf[:, 2, :], ALU.mult, [('tmpf', 2), ('yg', ft)], [('yg', ft)])
                        cp(yT[:, ft, :], ygrp[:, ft, :], [('yg', ft)], [('yT', ft)])
            if not full:
                return
            for fo in range(4):
                b = bank('acc')
                for kt in range(4):
                    mm(banks[b][:, 0:NT], wglu[:, kt, fo * 128:(fo + 1) * 128], yT[:, kt, :], kt == 0, kt == 3,
                       [('wglu', l), ('yT', kt)], [PB(b)])
                act(tmpf[:, 2, :], banks[b][:, 0:NT], AF.Sigmoid, [PB(b)], [('tmpf', 2)])
                tt(ygrp[:, 4 + fo, :], ygrp[:, fo, :], tmpf[:, 2, :], ALU.mult, [('tmpf', 2), ('yg', fo)], [('yg', 4 + fo)])
            group_norm_write(l, [4, 5, 6, 7], [0, 1, 2, 3], 512)

        def group_norm_write(l, gsrc, ydst, width):
            b = bank('aux')
            n = len(gsrc)
            for i, g in enumerate(gsrc):
                act(sq[:, i % 2, :], ygrp[:, g, :], AF.Square, [('yg', g)], [('sq', i % 2)])
                mm(banks[b][:, 0:NT], onesb[:, :], sq[:, i % 2, :], i == 0, i == n - 1, ['onesb', ('sq', i % 2)], [PB(b)])
            act(rstd[:, 2, :], banks[b][:, 0:NT], AF.Sqrt, [PB(b)], ['rstd2'], scale=1.0 / width, bias=EPS)
            rcp(rstd[:, 2, :], rstd[:, 2, :], ['rstd2'], ['rstd2'])
            for g, yd in zip(gsrc, ydst):
                stt(yT[:, yd, :], ygrp[:, g, :], prm[:, l, POG + yd:POG + yd + 1], rstd[:, 2, :], ALU.mult, ALU.mult,
                    [('yg', g), ('prm', l), 'rstd2'], [('yT', yd)])

        def conv(l, ti):
            for j in range(4):
                for si, (c0, ln, r, sid) in enumerate(ti['segs']):
                    zo = seg_zoff(ti, si)
                    o = ygrp[:, j, c0:c0 + ln]
                    rk = [('zT', j), ('prm', l)]
                    ts(o, zT[:, j, zo:zo + ln], prm[:, l, PCW + 3 * j:PCW + 3 * j + 1], None, ALU.mult, ALU.bypass, rk, [('yg', j)])
                    stt(o, zT[:, j, zo + 1:zo + 1 + ln], prm[:, l, PCW + 3 * j + 1:PCW + 3 * j + 2], o, ALU.mult, ALU.add,
                        rk + [('yg', j)], [('yg', j)])
                    stt(o, zT[:, j, zo + 2:zo + 2 + ln], prm[:, l, PCW + 3 * j + 2:PCW + 3 * j + 3], o, ALU.mult, ALU.add,
                        rk + [('yg', j)], [('yg', j)])
                tt(ygrp[:, j, :], ygrp[:, j, :], gbT[:, j, :], ALU.mult, [('yg', j), ('gbT', j)], [('yg', j)])
            group_norm_write(l, [0, 1, 2, 3], [12, 13, 14, 15], 512)

        def conv_carry(ti):
            for j in range(4):
                cp(zT[:, j, 0:2], zT[:, j, NT:NT + 2], [('zT', j)], [('zT', j)])

        def attention(l, ti):
            for si, (c0, ln, r, sid) in enumerate(ti['segs']):
                cb = si * 3 if not ti['prompt'] else 0
                nq = ln // 64
                for qc in range(nq):
                    q0 = c0 + qc * 64
                    for kv in range(2):
                        for j in range(3):
                            kc = cb + 2 + qc - j
                            for a in range(2):
                                b = bank('att')
                                mm(banks[b][0:64, 0:256].rearrange("p (t q) -> p t q", q=64),
                                   kdup[a * 64:(a + 1) * 64, kv, kc * 64:(kc + 1) * 64],
                                   qT[a * 64:(a + 1) * 64, 4 * kv:4 * kv + 4, q0:q0 + 64], True, True,
                                   [('kdup', si), ('qT', 4 * kv), ('qT', 4 * kv + 1), ('qT', 4 * kv + 2), ('qT', 4 * kv + 3)], [PB(b)])
                                tt(tS[:, j % 2, a * 256:(a + 1) * 256], banks[b][0:64, 0:256], biasT[:, j, kv, a * 256:(a + 1) * 256], ALU.add,
                                   [PB(b), 'biasT'], [('tS', j % 2)])
                            halo_first = ti['prompt'] and ti['first'] and (qc - j) < 0
                            if halo_first:
                                act(PT[:, kv, j, :], tS[:, j % 2, :], AF.Exp, [('tS', j % 2), 'flg'], [('PT', kv, j)],
                                    bias=flg[0:64, 12:13])
                            else:
                                act(PT[:, kv, j, :], tS[:, j % 2, :], AF.Exp, [('tS', j % 2)], [('PT', kv, j)])
                        bpv, bden = bank('att'), bank('aux')
                        for j in range(3):
                            kc = cb + 2 + qc - j
                            mm(banks[bpv][:, :], vtok[:, kc, kv, :], PT[:, kv, j, :], j == 0, j == 2, [('vtok', kc), ('PT', kv, j)], [PB(bpv)])
                        for j in range(3):
                            mm(banks[bden][:, :], onesb[0:64, :], PT[:, kv, j, :], j == 0, j == 2, ['onesb', ('PT', kv, j)], [PB(bden)])
                        tt(atmp[:, 0, :].rearrange("p (h q) -> p h q", q=64), banks[bden][:, :].rearrange("p (h q) -> p h q", q=64),
                           es16[:, kv * 8:(kv + 1) * 8].unsqueeze(2).to_broadcast([128, 8, 64]), ALU.add, [PB(bden), 'es16'], ['xstage'])
                        rcp(atmp[:, 0, :], atmp[:, 0, :], ['xstage'], ['xstage'])
                        for a in range(2):
                            pr = slice(a * 64, (a + 1) * 64)
                            tt(ygrp[pr, 4 * kv:4 * kv + 4, q0:q0 + 64],
                               banks[bpv][pr, a * 256:(a + 1) * 256].rearrange("p (t q) -> p t q", q=64),
                               atmp[pr, 0, a * 256:(a + 1) * 256].rearrange("p (t q) -> p t q", q=64), ALU.mult,
                               [PB(bpv), 'xstage'], [('yg', 4 * kv + t) for t in range(4)])
            group_norm_write(l, list(range(8)), list(range(4, 12)), 1024)

        def kv_carry(ti):
            for kv in range(2):
                cp(kdup[:, kv, 0:128], kdup[:, kv, (2 + NT // 64 - 2) * 64:(2 + NT // 64) * 64], [('kdup', 0)], [('kdup', 0)])
            n = NT // 64
            for c in range(2):
                cp(vtok[:, c].rearrange("p a b -> p (a b)"), vtok[:, n + c].rearrange("p a b -> p (a b)"),
                   [('vtok', n + c)], [('vtok', c)])

        def out_proj(l, ti):
            MK = ('mod', l)
            for ci in range(8):
                wv, wk = w_get('out', l, ci)
                for f in range(2):
                    ft = 2 * ci + f
                    b = bank('acc')
                    for kt in range(16):
                        mm(banks[b][:, 0:NT], wv[:, kt, f * 128:(f + 1) * 128], yT[:, kt, :], kt == 0, kt == 15,
                           [wk, ('yT', kt)], [PB(b)])
                    for (c0, ln, r, sid) in ti['segs']:
                        stt(xT[:, ft, c0:c0 + ln], banks[b][:, c0:c0 + ln], modT[:, l, 32 + ft, r:r + 1], xT[:, ft, c0:c0 + ln],
                            ALU.mult, ALU.add, [PB(b), MK, ('xT', ft)], [('xT', ft)])

        def ffn(l, ti):
            MK = ('mod', l)
            for gi in range(4):
                for cc in range(8):
                    wv, wk = w_get('ff1', l, gi * 8 + cc)
                    for f in range(2):
                        hf = cc * 2 + f
                        b = bank('acc')
                        for kt in range(16):
                            mm(banks[b][:, 0:NT], wv[:, kt, f * 128:(f + 1) * 128], hb[:, kt, :], kt == 0, kt == 15,
                               [wk, ('hb', kt)], [PB(b)])
                        act(tmpf[:, 3, :], banks[b][:, 0:NT], AF.Relu, [PB(b)], [('tmpf', 3)])
                        tt(hid[:, hf, :], tmpf[:, 3, :], tmpf[:, 3, :], ALU.mult, [('tmpf', 3)], [('hid', hf)])
                for oc in range(8):
                    wv, wk = w_get('ff2', l, gi * 8 + oc)
                    for f in range(2):
                        ft = 2 * oc + f
                        b = bank('acc')
                        for kt in range(16):
                            mm(banks[b][:, 0:NT], wv[:, kt, f * 128:(f + 1) * 128], hid[:, kt, :], kt == 0, kt == 15,
                               [wk, ('hid', kt)], [PB(b)])
                        for (c0, ln, r, sid) in ti['segs']:
                            stt(xT[:, ft, c0:c0 + ln], banks[b][:, c0:c0 + ln], modT[:, l, 80 + ft, r:r + 1], xT[:, ft, c0:c0 + ln],
                                ALU.mult, ALU.add, [PB(b), MK, ('xT', ft)], [('xT', ft)])

        def install_halo(ti, si, kn_ap, v_ap, z_ap, rk):
            cb = si * 3 if not ti['prompt'] else 0
            cp(knb[:, 0:128], kn_ap, rk + ['knb'], ['knb'])
            for kv in range(2):
                b2 = bank('aux')
                mm(banks[b2][:, 0:128], selk[:, kv, :], knb[:, 0:128], True, True, ['selk', 'knb'], [PB(b2)])
                act(kdup[:, kv, cb * 64:cb * 64 + 128], banks[b2][:, 0:128], AF.Copy, [PB(b2)], [('kdup', si)])
            for c in range(2):
                for dup in range(2):
                    act(vtok[:, cb + c, :, dup * 64:(dup + 1) * 64], v_ap[:, c, :].rearrange("p (k d) -> p k d", k=2), AF.Copy,
                        rk, [('vtok', cb + c)])
            zo = seg_zoff(ti, si)
            for j in range(4):
                cp(zT[:, j, zo:zo + 2], z_ap[:, j, :], rk + [('zT', j)], [('zT', j)])

        def load_sample_state(l, ti):
            for si in range(4):
                dma(hal[:, 32:160], ck[l, si].rearrange("t f -> f t"), ['hal'], ['hal'], slow=True)
                dma(hal[0:64, 160:416].rearrange("p (c f) -> p c f", c=2), cv[l, si].rearrange("(c k) f -> k c f", c=2), ['hal'], ['hal'])
                for j_ in range(4):
                    dma(hal[:, 416 + 2 * j_:418 + 2 * j_], scv[l, si].rearrange("t (j p) -> p j t", p=128)[:, j_, :],
                        ['hal'], ['hal'], slow=True)
                install_halo(ti, si, hal[:, 32:160], hal[0:64, 160:416].rearrange("p (c f) -> p c f", c=2),
                             hal[:, 416:424].rearrange("p (j t) -> p j t", t=2), ['hal'])
                for s_ in range(16):
                    pass
                dma(car[:, 1 + si, 0, :], sre[l, si].rearrange("(s q) -> q s", q=128), [('car', 1 + si, s) for s in range(16)],
                    [('car', 1 + si, s) for s in range(16)], slow=True)
                dma(car[:, 1 + si, 1, :], sim[l, si].rearrange("(s q) -> q s", q=128), [('car', 1 + si, s) for s in range(16)],
                    [('car', 1 + si, s) for s in range(16)], slow=True)

        def exchange(l, ti_last):
            CK5 = [('car', 5, s) for s in range(16)]
            CK0 = [('car', 0, s) for s in range(16)]
            cp(exb[:, 0:16], car[:, 5, 0, :], CK5, ['exb'])
            cp(exb[:, 16:32], car[:, 5, 1, :], CK5 + ['exb'], ['exb'])
            cp(exb[:, 32:160], knf[:, NT - 128:NT], ['knf', 'exb'], ['exb'])
            mset(exb[64:128, 160:416], 0.0, ['exb'], ['exb'])
            n = NT // 64
            cp(exb[0:64, 160:416].rearrange("p (c f) -> p c f", c=2), vf[:, n - 2:n, :], [('vf', n - 2), ('vf', n - 1), 'exb'], ['exb'])
            cp(exb[:, 416:424].rearrange("p (j t) -> p j t", t=2), zT[:, :, NT:NT + 2], [('zT', j) for j in range(4)] + ['exb'], ['exb'])
            mset(exb[:, 424:EXW], 0.0, ['exb'], ['exb'])
            dma(cc_in[l][:, :], exb[:], ['exb'], [('ccin', l)])
            S.add('pool', lambda e: e.collective_compute("AllGather", ALU.bypass, replica_groups=[[0, 1, 2, 3], [4, 5, 6, 7]],
                                                         ins=[cc_in[l].ap().opt()], outs=[cc_out[l].ap().opt()]),
                  [('ccin', l)], [('ccout', l)])
            dma(gth[:], cc_out[l].ap().rearrange("(r p) c -> p r c", p=128), [('ccout', l), 'xstage'], ['xstage'])
            SK = [('ssmt', i) for i in range(4)]
            for m in range(3):
                o = ssmt[:, m, 0:32]
                ts(o, gth[:, 0, 0:32], flg[:, 4 * m:4 * m + 1], None, ALU.mult, ALU.bypass, ['xstage', 'flg', SK[m]], [SK[m]])
                for r in range(1, 4):
                    stt(o, gth[:, r, 0:32], flg[:, 4 * m + r:4 * m + r + 1], o, ALU.mult, ALU.add, ['xstage', 'flg', SK[m]], [SK[m]])
            AK = ('a4k', l)
            cp(car[:, 0, 0, :], ssmt[:, 0, 0:16], [SK[0]] + CK0, CK0)
            cp(car[:, 0, 1, :], ssmt[:, 0, 16:32], [SK[0]] + CK0, CK0)
            for m in (1, 2):
                xr, xi = ssmt[:, m, 0:16], ssmt[:, m, 16:32]
                ar_, ai_ = a4k[:, 2 * (m - 1), :], a4k[:, 2 * (m - 1) + 1, :]
                t0, t1 = ssmt[:, 3, 0:16], ssmt[:, 3, 16:32]
                tt(t0, xr, ar_, ALU.mult, [SK[m], AK, SK[3]], [SK[3]])
                tt(t1, xi, ai_, ALU.mult, [SK[m], AK, SK[3]], [SK[3]])
                tt(t0, t0, t1, ALU.subtract, [SK[3]], [SK[3]])
                tt(car[:, 0, 0, :], car[:, 0, 0, :], t0, ALU.add, [SK[3]] + CK0, CK0)
                tt(t0, xr, ai_, ALU.mult, [SK[m], AK, SK[3]], [SK[3]])
                tt(t1, xi, ar_, ALU.mult, [SK[m], AK, SK[3]], [SK[3]])
                tt(t0, t0, t1, ALU.add, [SK[3]], [SK[3]])
                tt(car[:, 0, 1, :], car[:, 0, 1, :], t0, ALU.add, [SK[3]] + CK0, CK0)
            o = hal[:, 32:424]
            ts(o, gth[:, 0, 32:424], flg[:, 0:1], None, ALU.mult, ALU.bypass, ['xstage', 'flg', 'hal'], ['hal'])
            for r in range(1, 4):
                stt(o, gth[:, r, 32:424], flg[:, r:r + 1], o, ALU.mult, ALU.add, ['xstage', 'flg', 'hal'], ['hal'])
            pt0 = tile_info(0)
            install_halo(pt0, 0, hal[:, 32:160], hal[0:64, 160:416].rearrange("p (c f) -> p c f", c=2),
                         hal[:, 416:424].rearrange("p (j t) -> p j t", t=2), ['hal'])

        def esink_setup(l):
            act(es16[:], es16[:], AF.Exp, ['es16'], ['es16'])

        def emit_state_outputs(l, ti):
            if ti['prompt']:
                specs = [(0, 0, NT, nkp[l], nvp[l], srp[l], sip[l], cvp[l], None)]
            else:
                specs = [(si, c0, ln, nks[l, si], nvs[l, si], srs[l, si], sis[l, si], cvs[l, si], si) for si, (c0, ln, r, sid) in enumerate(ti['segs'])]
            for (si, c0, ln, dk, dv, dr, di, dc, smp) in specs:
                sid = ti['segs'][si][3]
                CK = [('car', sid, s) for s in range(16)]
                dma(dr.rearrange("(s q) -> q s", q=128), car[:, sid, 0, :], CK, ['sout'], slow=True)
                dma(di.rearrange("(s q) -> q s", q=128), car[:, sid, 1, :], CK, ['sout'], slow=True)
                zo = seg_zoff(ti, si)
                for j_ in range(4):
                    dma(dc.rearrange("t (j p) -> p j t", p=128)[:, j_, :], zT[:, j_, zo + ln:zo + ln + 2], [('zT', j_)], ['sout'], slow=True)
                ntr = 2 if smp is None else 1
                for i in range(ntr):
                    cc0 = c0 + ln - 64 * (ntr - i)
                    b = bank('aux')
                    tr(banks[b][0:64, 0:128], knf[:, cc0:cc0 + 64], ident[:], ['knf', 'ident'], [PB(b)])
                    cp(o64[:, i, :], banks[b][0:64, 0:128], [PB(b)], [('o64', i)])
                    row0 = 64 * i if smp is None else 64
                    dma(dk[row0:row0 + 64, :], o64[:, i, :], [('o64', i)], ['sout'])
                    ch = cc0 // 64
                    dma(dv[row0:row0 + 64, :], vf[:, ch, :], [('vf', ch)], ['sout'])
                if smp is not None:
                    dma(dk[0:64, :], ck[l, si, 64:128, :], [], ['sout'])
                    dma(dv[0:64, :], cv[l, si, 64:128, :], [], ['sout'])

        def p1_chunks(l, last):
            seq = [('in', l, 0), ('in', l, 1)]
            if last:
                seq += [('in', l, c) for c in (6, 7, 8, 11, 12, 9, 10)]
            return seq

        def p2_chunks(l):
            seq = [('in', l, c) for c in (0, 1, 2, 3, 4, 5, 6, 7, 8, 11, 12, 9, 10)]
            seq += [('out', l, c) for c in range(8)]
            for gi in range(4):
                seq += [('ff1', l, gi * 8 + c) for c in range(8)]
                seq += [('ff2', l, gi * 8 + c) for c in range(8)]
            return seq

        def P1(l, ti):
            dbg = (l == 0 and ti['t'] == 0)
            if dbg:
                checkpoint()
            norm_mod(l, ti, 0)
            if dbg:
                checkpoint()
            parts = ['u'] + (['kv', 'conv'] if ti['last'] else [])
            in_proj(l, ti, parts)
            if dbg:
                checkpoint()
            ssm(l, ti, False, lambda sid: 5)

        def P2(l, ti):
            dbg = (l == 0 and ti['t'] == 0)
            norm_mod(l, ti, 0)
            in_proj(l, ti, ['u', 'q', 'kv', 'conv'])
            if dbg:
                checkpoint()
            ssm(l, ti, True, lambda sid: sid)
            if dbg:
                checkpoint()
            conv(l, ti)
            if dbg:
                checkpoint()
            attention(l, ti)
            if dbg:
                checkpoint()
            if ti['last']:
                emit_state_outputs(l, ti)
            elif ti['prompt']:
                kv_carry(ti)
                conv_carry(ti)
            out_proj(l, ti)
            if dbg:
                checkpoint()
            norm_mod(l, ti, 1)
            ffn(l, ti)

        tiles = [tile_info(t) for t in range(NPT + 1)]
        for l in range(2):
            plan.extend([('ada', l, c) for c in range(48)])
        for t in range(NPT):
            plan.extend(p1_chunks(0, t == NPT - 1))
        for t in range(NPT + 1):
            plan.extend(p2_chunks(0))
        for t in range(NPT):
            plan.extend(p1_chunks(1, t == NPT - 1))
        for t in range(NPT + 1):
            plan.extend(p2_chunks(1))

        def zero_p1_carry():
            mset(car[:, 5].rearrange("p a b -> p (a b)"), 0.0, [('car', 5, s) for s in range(16)], [('car', 5, s) for s in range(16)])

        def main_seq():
            cast_all('ada', 0)
            cast_all('ada', 1)
            cast_all('in', 0)
            load_layer_params(0)
            load_ssm(0)
            cast_all('out', 0)
            cast_all('ff1', 0)
            cast_all('ff2', 0)
            cast_all('in', 1)
            cast_all('out', 1)
            cast_all('ff1', 1)
            cast_all('ff2', 1)
            checkpoint()
            compute_mod(0)
            compute_mod(1)
            esink_setup(0)
            checkpoint()
            zero_p1_carry()
            for t in range(NPT):
                load_x0(tiles[t])
                P1(0, tiles[t])
                if t == 0:
                    checkpoint()
            checkpoint()
            exchange(0, tiles[NPT - 1])
            checkpoint()
            for t in range(NPT + 1):
                ti = tiles[t]
                load_x0(ti)
                if not ti['prompt']:
                    load_sample_state(0, ti)
                P2(0, ti)
                store_x1(ti)
                if t == 0:
                    checkpoint()
                if t == NPT - 1:
                    checkpoint()
            checkpoint()
            load_layer_params(1)
            load_ssm(1)
            esink_setup(1)
            zero_p1_carry()
            checkpoint()
            for t in range(NPT):
                load_x1(tiles[t])
                P1(1, tiles[t])
            checkpoint()
            exchange(1, tiles[NPT - 1])
            checkpoint()
            for t in range(NPT + 1):
                ti = tiles[t]
                load_x1(ti)
                if not ti['prompt']:
                    load_sample_state(1, ti)
                P2(1, ti)
                store_y(ti)

        try:
            main_seq()
        except _Stop:
            pass
        for e_ in ('sp', 'pool'):
            op_ = S.add(e_, None, reads=['yout', 'sout'])
            for o_ in S.slot_last:
                if o_ is not None:
                    op_.deps.add(o_)
        S.emit(block, sems, dsem)
    return nc


def _bucket(rel):
    half, exact = 16, 8
    n = np.abs(rel)
    nf = np.maximum(n, 1).astype(np.float32)
    far = exact + (np.log(nf / exact) / math.log(64 / exact) * (half - exact)).astype(np.int32)
    far = np.minimum(far, half - 1)
    return np.where(rel > 0, half, 0) + np.where(n < exact, n, far)


_NC_CACHE = {}


def kernel(**inp):
    f32 = np.float32
    inp = {k: np.ascontiguousarray(np.asarray(v), dtype=f32) for k, v in inp.items()}
    if 'nc' not in _NC_CACHE:
        _NC_CACHE['nc'] = build_program()
    nc = _NC_CACHE['nc']
    e = np.arange(256)
    rel = 63 - e
    bk = _bucket(rel)
    ohe = np.zeros((32, 256), f32)
    ohe[bk, e] = 1.0
    ident = np.eye(128, dtype=f32)
    iota = np.arange(1, LC + 1, dtype=f32)[None, :]
    shared = ['rel_bias', 'w_ada', 'b_ada', 'w_in', 'ssm_log_dt', 'ssm_b_re', 'ssm_b_im', 'ssm_c_re', 'ssm_c_im', 'ssm_d',
              'ssm_w_glu', 'q_norm_g', 'k_norm_g', 'attn_sinks', 'conv_w', 'out_norm_g', 'w_out', 'w_ff1', 'w_ff2']
    maps = []
    for c in range(8):
        b, j = c // 4, c % 4
        m = {k: inp[k] for k in shared}
        m['ssm_a_re'] = inp['ssm_a_re'].reshape(2, 2048)
        m['ssm_a_im'] = inp['ssm_a_im'].reshape(2, 2048)
        m['xp'] = inp['x_prompt'][b, j * 4096:(j + 1) * 4096]
        m['xs'] = inp['x_sample'][4 * c:4 * c + 4].reshape(256, 2048)
        m['ck'] = inp['cache_k'][:, 4 * c:4 * c + 4].reshape(2, 4, 128, 128)
        m['cv'] = inp['cache_v'][:, 4 * c:4 * c + 4].reshape(2, 4, 128, 128)
        m['sre'] = inp['state_ssm_re'][:, 4 * c:4 * c + 4].reshape(2, 4, 2048)
        m['sim'] = inp['state_ssm_im'][:, 4 * c:4 * c + 4].reshape(2, 4, 2048)
        m['scv'] = inp['state_conv'][:, 4 * c:4 * c + 4]
        m['cvec'] = np.concatenate([inp['c_prompt'][b:b + 1], inp['c_sample'][4 * c:4 * c + 4]], 0)
        fl = np.zeros((1, 16), f32)
        for mm_ in range(3):
            r = j - 1 - mm_
            if r >= 0:
                fl[0, 4 * mm_ + r] = 1.0
        fl[0, 12] = 0.0 if j > 0 else -30000.0
        m['flags'] = fl
        m['ident'] = ident
        m['iota'] = iota
        m['ohe'] = ohe
        maps.append({k: np.ascontiguousarray(v) for k, v in m.items()})
    res = run_bass_kernel_spmd(nc, maps, core_ids=list(range(8)))
    R = res.results
    y_p = np.zeros((2, 16384, 2048), f32)
    for c in range(8):
        y_p[c // 4, (c % 4) * 4096:(c % 4 + 1) * 4096] = R[c]['yp']
    y_s = np.concatenate([R[c]['ys'].reshape(4, 64, 2048) for c in range(8)], 0)
    lastc = [3, 7]
    nk_p = np.stack([R[c]['nkp'] for c in lastc], 1).reshape(2, 2, 128, 2, 64)
    nv_p = np.stack([R[c]['nvp'] for c in lastc], 1).reshape(2, 2, 128, 2, 64)
    sr_p = np.stack([R[c]['srp'] for c in lastc], 1).reshape(2, 2, 32, 64)
    si_p = np.stack([R[c]['sip'] for c in lastc], 1).reshape(2, 2, 32, 64)
    cv_p = np.stack([R[c]['cvp'] for c in lastc], 1).reshape(2, 2, 2, 512)
    nk_s = np.concatenate([R[c]['nks'] for c in range(8)], 1).reshape(2, 32, 128, 2, 64)
    nv_s = np.concatenate([R[c]['nvs'] for c in range(8)], 1).reshape(2, 32, 128, 2, 64)
    sr_s = np.concatenate([R[c]['srs'] for c in range(8)], 1).reshape(2, 32, 32, 64)
    si_s = np.concatenate([R[c]['sis'] for c in range(8)], 1).reshape(2, 32, 32, 64)
    cv_s = np.concatenate([R[c]['cvs'] for c in range(8)], 1).reshape(2, 32, 2, 512)
    outs = (y_p, y_s, nk_p, nv_p, sr_p, si_p, cv_p, nk_s, nv_s, sr_s, si_s, cv_s)
    return tuple(np.ascontiguousarray(o, dtype=f32) for o in outs)


def _stats():
    import time
    t0 = time.time()
    nc = build_program()
    print('build', time.time() - t0)
    return nc
```

```python
import math
import numpy as np
from contextlib import ExitStack
import concourse.bass as bass
import concourse.mybir as mybir
from concourse.bass_utils import run_bass_kernel_spmd

F32 = mybir.dt.float32
BF16 = mybir.dt.bfloat16
I32 = mybir.dt.int32
AF = mybir.ActivationFunctionType
ALU = mybir.AluOpType
ENGS = ['pe', 'act', 'dve', 'pool', 'sp']
EPS = 1e-6
NT = 256
NPT = 16
LC = 64
TWO_PI = 2.0 * math.pi


class Op:
    __slots__ = ('eng', 'fn', 'deps', 'sig', 'dma', 'slot', 'use', 'cnt')

    def __init__(self, eng, fn, dma):
        self.eng = eng
        self.fn = fn
        self.deps = set()
        self.sig = False
        self.dma = dma
        self.slot = -1
        self.use = 0
        self.cnt = 0


class Sched:
    def __init__(self, n_dma_sems=40):
        self.ops = {e: [] for e in ENGS}
        self.res = {}
        self.n_dma = 0
        self.P = n_dma_sems
        self.slot_last = [None] * n_dma_sems
        self.qpools = {'sp': (0, 16), 'pool': (16, 3), 'act': (19, 4)}
        self.qn = {}

    def add(self, eng, fn, reads=(), writes=(), dma=False):
        op = Op(eng, fn, dma)
        deps = op.deps
        psr = [k for k in reads if isinstance(k, tuple) and k[0] == 'ps']
        if psr:
            reads = [k for k in reads if k not in psr]
            writes = list(writes) + [k for k in psr if k not in writes]
        for k in reads:
            r = self.res.get(k)
            if r is not None and r[0] is not None:
                deps.add(r[0])
        for k in writes:
            r = self.res.get(k)
            if r is not None:
                if r[0] is not None:
                    deps.add(r[0])
                deps.update(r[1])
        for k in reads:
            r = self.res.get(k)
            if r is None:
                r = [None, []]
                self.res[k] = r
            if not dma:
                r[1] = [o for o in r[1] if o.dma or o.eng != eng]
            r[1].append(op)
        for k in writes:
            self.res[k] = [op, []]
        if dma:
            base, cnt = self.qpools[eng]
            n = self.qn.get(eng, 0)
            s = base + n % cnt
            op.slot = s
            op.use = n // cnt + 1
            if self.slot_last[s] is not None:
                deps.add(self.slot_last[s])
            self.slot_last[s] = op
            self.qn[eng] = n + 1
            self.n_dma += 1
        deps.discard(op)
        self.ops[eng].append(op)
        return op

    EPOCH = 12000

    def emit(self, block, sems, dma_sems):
        for e in ENGS:
            for op in self.ops[e]:
                for d in op.deps:
                    if d.dma:
                        continue
                    if d.eng == 'pe' and op.eng == 'pe' and not op.dma:
                        continue
                    d.sig = True
        for e in ENGS:
            c = 0
            for op in self.ops[e]:
                if op.sig and not op.dma:
                    c += 1
                op.cnt = c
            assert c <= self.EPOCH * len(sems[e]), (e, c)

        EP = self.EPOCH

        def run(e, eng):
            waited = {}
            for op in self.ops[e]:
                for d in op.deps:
                    if d.dma:
                        key = ('d', d.slot)
                        sem = dma_sems[d.slot]
                        val = 16 * d.use
                    else:
                        if d.eng == e and e == 'pe' and not op.dma:
                            continue
                        ep = (d.cnt - 1) // EP
                        key = (d.eng, ep)
                        sem = sems[d.eng][ep]
                        val = d.cnt - ep * EP
                    if waited.get(key, 0) >= val:
                        continue
                    eng.wait_ge(sem, val)
                    waited[key] = val
                if op.fn is None:
                    continue
                inst = op.fn(eng)
                if op.dma:
                    inst.then_inc(dma_sems[op.slot], 16)
                elif op.sig:
                    inst.then_inc(sems[e][(op.cnt - 1) // EP], 1)

        @block.tensor
        def _(eng):
            run('pe', eng)

        @block.scalar
        def _(eng):
            run('act', eng)

        @block.vector
        def _(eng):
            run('dve', eng)

        @block.gpsimd
        def _(eng):
            run('pool', eng)

        @block.sync
        def _(eng):
            run('sp', eng)


MATS = {'in': (13, 16), 'out': (8, 16), 'ff1': (32, 16), 'ff2': (32, 16), 'ada': (48, 16)}


class _Stop(Exception):
    pass


def build_program(kstop=None):
    nc = bass.Bass("TRN2", target_bir_lowering=False)
    S = Sched()
    stage = [0]

    def checkpoint():
        stage[0] += 1
        if kstop is not None and stage[0] >= kstop:
            raise _Stop()

    def din(name, shape, dt=F32):
        return nc.dram_tensor(name, list(shape), dt, kind="ExternalInput").ap()

    def dout(name, shape, dt=F32):
        return nc.dram_tensor(name, list(shape), dt, kind="ExternalOutput").ap()

    xp = din("xp", [NPT * NT, 2048])
    xs = din("xs", [256, 2048])
    ck = din("ck", [2, 4, 128, 128])
    cv = din("cv", [2, 4, 128, 128])
    sre = din("sre", [2, 4, 2048])
    sim = din("sim", [2, 4, 2048])
    scv = din("scv", [2, 4, 2, 512])
    cvec = din("cvec", [5, 2048])
    flags = din("flags", [1, 16])
    ident_d = din("ident", [128, 128])
    iota_d = din("iota", [1, LC])
    ohe_d = din("ohe", [32, 256])
    rel_bias = din("rel_bias", [32, 16])
    w_ada = din("w_ada", [2, 2048, 12288])
    b_ada = din("b_ada", [2, 12288])
    w_in = din("w_in", [2, 2048, 3328])
    a_re_d = din("ssm_a_re", [2, 2048])
    a_im_d = din("ssm_a_im", [2, 2048])
    log_dt_d = din("ssm_log_dt", [2, 32])
    b_re_d = din("ssm_b_re", [2, 32, 64, 16])
    b_im_d = din("ssm_b_im", [2, 32, 64, 16])
    c_re_d = din("ssm_c_re", [2, 32, 16, 64])
    c_im_d = din("ssm_c_im", [2, 32, 16, 64])
    ssm_d_d = din("ssm_d", [2, 512])
    w_glu_d = din("ssm_w_glu", [2, 512, 512])
    q_g_d = din("q_norm_g", [2, 64])
    k_g_d = din("k_norm_g", [2, 64])
    sinks_d = din("attn_sinks", [2, 16])
    conv_w_d = din("conv_w", [2, 512, 3])
    out_g_d = din("out_norm_g", [2, 2048])
    w_out = din("w_out", [2, 2048, 2048])
    w_ff1 = din("w_ff1", [2, 2048, 8192])
    w_ff2 = din("w_ff2", [2, 8192, 2048])

    yp = dout("yp", [NPT * NT, 2048])
    ys = dout("ys", [256, 2048])
    nkp = dout("nkp", [2, 128, 128])
    nvp = dout("nvp", [2, 128, 128])
    srp = dout("srp", [2, 2048])
    sip = dout("sip", [2, 2048])
    cvp = dout("cvp", [2, 2, 512])
    nks = dout("nks", [2, 4, 128, 128])
    nvs = dout("nvs", [2, 4, 128, 128])
    srs = dout("srs", [2, 4, 2048])
    sis = dout("sis", [2, 4, 2048])
    cvs = dout("cvs", [2, 4, 2, 512])
    out_keys = []

    wsc = {}
    for m, (nch, _) in MATS.items():
        for l in range(2):
            wsc[(m, l)] = nc.dram_tensor("wsc_%s%d" % (m, l), [nch, 128, 4096], BF16)
    x1s = nc.dram_tensor("x1s", [NPT + 1, 128, 16 * NT], F32)
    tvd = nc.dram_tensor("tvd", [16, 256], F32)
    EXW = 432
    cc_in = [nc.dram_tensor("cc_in%d" % l, [128, EXW], F32) for l in range(2)]
    cc_out = [nc.dram_tensor("cc_out%d" % l, [512, EXW], F32) for l in range(2)]

    with ExitStack() as es:
        def sb(name, shape, dt=F32):
            return es.enter_context(nc.sbuf_tensor(name + "_sb", list(shape), dt))

        xT = sb("xT", [128, 16, NT])
        hb = sb("hb", [128, 16, NT], BF16)
        ygrp = sb("ygrp", [128, 8, NT])
        yT = sb("yT", [128, 16, NT], BF16)
        qT = sb("qT", [128, 8, NT], BF16)
        uT = sb("uT", [128, 4, NT])
        uTb = sb("uTb", [128, 4, NT], BF16)
        gbT = sb("gbT", [128, 4, NT])
        ZW = 4 * (2 + 64)
        zT = sb("zT", [128, 4, 2 + NT + 8])
        knf = sb("knf", [128, NT])
        knb = sb("knb", [128, NT], BF16)
        kdup = sb("kdup", [128, 2, 12 * 64], BF16)
        vtok = sb("vtok", [64, 12, 2, 128], BF16)
        vf = sb("vf", [64, 4, 128])
        hid = sb("hid", [128, 16, NT], BF16)
        wring = sb("wring", [128, 3, 4096], BF16)
        xstage = sb("xstage", [128, 2048])
        ident = sb("identsb", [128, 128])
        identb = sb("identb", [128, 128], BF16)
        onesb = sb("onesb", [128, 128], BF16)
        ones64 = sb("ones64", [128, 128], BF16)
        selk = sb("selk", [128, 2, 128], BF16)
        flg = sb("flg", [128, 16])
        iota = sb("iota", [128, LC])
        sq = sb("sq", [128, 2, NT], BF16)
        rstd = sb("rstd", [128, 3, NT])
        tmpf = sb("tmpf", [128, 4, NT])
        modT = sb("modT", [128, 2, 96, 5])
        sc1p = sb("sc1p", [128, 2, 2, 16, 5])
        scT = sb("scT", [128, 16, 5], BF16)
        cT = sb("cT", [128, 16, 5])
        badaT = sb("badaT", [128, 2, 96])
        prm = sb("prm", [128, 2, 64])
        PQG, PKG, POG, PSD, PCW = 0, 1, 2, 18, 22
        sst = sb("sst", [128, 24, 16])
        tabE = sb("tabE", [128, 2, 16, LC])
        tabW = sb("tabW", [128, 2, 16, LC])
        fBw = sb("fBw", [128, 16, 2, 128], BF16)
        Cw = sb("Cw", [128, 16, 2, 128], BF16)
        bcst = sb("bcst", [128, 2, 128])
        wglu = sb("wglu", [128, 4, 512], BF16)
        biasT = sb("biasT", [64, 3, 2, 512])
        es16 = sb("es16", [128, 16])
        tvs = sb("tvs", [16, 256])
        tperm = sb("tperm", [32, 16])
        ohe = sb("ohe", [32, 256])
        PT = sb("PT", [64, 2, 3, 512], BF16)
        tS = sb("tS", [64, 2, 512])
        car = sb("car", [128, 6, 2, 16])
        ssmt = sb("ssmt", [128, 8, NT])
        hrb = sb("hrb", [128, 2, NT], BF16)
        exb = sb("exb", [128, EXW])
        hal = sb("hal", [128, EXW])
        a4k = sb("a4k", [128, 6, 16])
        fence = sb("fence", [128, 4])
        o64 = sb("o64", [64, 2, 128])
        tang = ssmt[:, 0:4, :].rearrange("p a b -> p (a b)")
        tang2 = ssmt[:, 4:8, :].rearrange("p a b -> p (a b)")
        tangi = xstage[:, 1024:2048].bitcast(I32)
        atmp = xstage[:, :].rearrange("p (a b) -> p a b", a=4)
        gth = xstage[:, 0:4 * EXW].rearrange("p (r c) -> p r c", r=4)
        TANG = [('ssmt', i) for i in range(4)]
        TANG2 = [('ssmt', i) for i in range(4, 8)]

        banks = [es.enter_context(nc.psum_tensor("bank%d" % i, [128, 512], F32)) for i in range(8)]
        NEP = {"pe": 3, "act": 2, "dve": 8, "pool": 1, "sp": 1}
        sems = {e: [es.enter_context(nc.semaphore("s_%s%d" % (e, i))) for i in range(NEP[e])] for e in ENGS}
        dsem = [es.enter_context(nc.semaphore("d%d" % i)) for i in range(S.P)]
        block = es.enter_context(nc.Block())

        rot = {'acc': [0, [0, 1, 2]], 'aux': [0, [3, 4]], 'att': [0, [5, 6, 7]]}

        def bank(kind):
            r = rot[kind]
            b = r[1][r[0] % len(r[1])]
            r[0] += 1
            return b

        def PB(b):
            return ('ps', b)

        def mm(out, lhsT, rhs, start, stop, reads, writes):
            S.add('pe', lambda e: e.matmul(out, lhsT=lhsT, rhs=rhs, start=start, stop=stop), reads, writes)

        def tr(out, in_, idn, reads, writes):
            S.add('pe', lambda e: e.transpose(out=out, in_=in_, identity=idn), reads, writes)

        def act(out, in_, func, reads, writes, scale=1.0, bias=0.0):
            S.add('act', lambda e: e.activation(out=out, in_=in_, func=func, scale=scale, bias=bias), reads, writes)

        def tt(out, a, b, op, reads, writes, eng='dve'):
            S.add(eng, lambda e: e.tensor_tensor(out=out, in0=a, in1=b, op=op), reads, writes)

        def ts(out, a, s1, s2, op0, op1, reads, writes, eng='dve'):
            S.add(eng, lambda e: e.tensor_scalar(out=out, in0=a, scalar1=s1, scalar2=s2, op0=op0, op1=op1), reads, writes)

        def stt(out, a, sc, b, op0, op1, reads, writes):
            S.add('dve', lambda e: e.scalar_tensor_tensor(out=out, in0=a, scalar=sc, in1=b, op0=op0, op1=op1), reads, writes)

        def cp(out, in_, reads, writes, eng='dve'):
            S.add(eng, lambda e: e.tensor_copy(out=out, in_=in_), reads, writes)

        def rcp(out, in_, reads, writes):
            S.add('dve', lambda e: e.reciprocal(out=out, in_=in_), reads, writes)

        def scan(out, d0, d1, init, reads, writes):
            S.add('dve', lambda e: e.tensor_tensor_scan(out=out, data0=d0, data1=d1, initial=init,
                                                        op0=ALU.mult, op1=ALU.add), reads, writes)

        def mset(ap, val, reads, writes, eng='dve'):
            S.add(eng, lambda e: e.memset(ap, val), reads, writes)

        def dma(out, in_, reads, writes, eng='sp', slow=False):
            if slow:
                S.add(eng, lambda e: e.dma_start(out=out, in_=in_, allow_slow_non_contiguous=True), reads, writes, dma=True)
            else:
                S.add(eng, lambda e: e.dma_start(out=out, in_=in_), reads, writes, dma=True)

        def src_view(m, l, ci):
            if m == 'in':
                c0 = ci * 256
                return w_in[l].rearrange("(kt p) c -> p kt c", p=128)[:, :, c0:c0 + 256]
            if m == 'out':
                return w_out[l].rearrange("(kt p) c -> p kt c", p=128)[:, :, ci * 256:(ci + 1) * 256]
            if m == 'ff1':
                return w_ff1[l].rearrange("(kt p) c -> p kt c", p=128)[:, :, ci * 256:(ci + 1) * 256]
            if m == 'ada':
                return w_ada[l].rearrange("(kt p) c -> p kt c", p=128)[:, :, ci * 256:(ci + 1) * 256]
            if m == 'ff2':
                gi, oc = ci // 8, ci % 8
                return w_ff2[l][gi * 2048:(gi + 1) * 2048, :].rearrange("(kt p) c -> p kt c", p=128)[:, :, oc * 256:(oc + 1) * 256]

        def cast_all(m, l):
            for ci in range(MATS[m][0]):
                dst = wsc[(m, l)][ci].rearrange("p (kt c) -> p kt c", kt=16)
                dma(dst, src_view(m, l, ci), [], [('wsc', m, l, ci)], eng='pool')

        plan = []
        plan_pos = [0, 0]

        def w_topup():
            while plan_pos[1] < len(plan) and plan_pos[1] < plan_pos[0] + 3:
                n = plan_pos[1]
                m, l, ci = plan[n]
                slot = n % 3
                dma(wring[:, slot, :], wsc[(m, l)][ci], [('wsc', m, l, ci)], [('w', slot)])
                plan_pos[1] += 1

        def w_get(m, l, ci):
            n = plan_pos[0]
            assert plan[n] == (m, l, ci), (plan[n], (m, l, ci))
            w_topup()
            plan_pos[0] += 1
            slot = n % 3
            return wring[:, slot, :].rearrange("p (kt c) -> p kt c", kt=16), ('w', slot)

        dma(ident[:], ident_d[:, :], [], ['ident'])
        dma(flg[:], flags[0:1, :].partition_broadcast(128), [], ['flg'])
        dma(iota[:], iota_d[0:1, :].partition_broadcast(128), [], ['iota'])
        dma(ohe[:], ohe_d[:, :], [], ['ohe'])
        for kv_ in range(2):
            for a_ in range(2):
                d0 = kv_ * 8 + a_ * 4
                s0 = kv_ * 8 + a_
                dma(tperm[:, d0:d0 + 4], rel_bias[:, kv_ * 8:kv_ * 8 + 8].rearrange("b (t a) -> b t a", a=2)[:, :, a_], [], ['tperm'], slow=True)
        cp(identb[:], ident[:], ['ident'], ['identb'])
        mset(onesb[:], 1.0, [], ['onesb'])
        mset(ones64[:], 0.0, [], ['ones64'])
        mset(ones64[0:64, 0:64], 1.0, ['ones64'], ['ones64'])
        mset(ones64[64:128, 64:128], 1.0, ['ones64'], ['ones64'])
        mset(selk[:], 0.0, [], ['selk'])
        for kv in range(2):
            for hf in range(2):
                cp(selk[kv * 64:(kv + 1) * 64, kv, hf * 64:(hf + 1) * 64], ident[kv * 64:(kv + 1) * 64, kv * 64:(kv + 1) * 64],
                   ['ident', 'selk'], ['selk'])
        mset(fence[:], 0.0, [], ['fence'])
        for r_ in range(5):
            dma(cT[:, :, r_], cvec[r_].rearrange("(kt p) -> p kt", p=128), [], ['cT'], slow=True)
        act(tmpf[:, 0, 0:80], cT[:].rearrange("p a b -> p (a b)"), AF.Sigmoid, ['cT'], ['tmpf0'])
        tt(scT[:].rearrange("p a b -> p (a b)"), tmpf[:, 0, 0:80], cT[:].rearrange("p a b -> p (a b)"), ALU.mult,
           ['tmpf0', 'cT'], ['scT'])
        for l in range(2):
            dma(badaT[:, l, :], b_ada[l].rearrange("(ft p) -> p ft", p=128), [], [('bada', l)], slow=True)

        b0 = bank('aux')
        mm(banks[b0][0:16, 0:256], tperm[:, :], ohe[:, :], True, True, ['tperm', 'ohe'], [PB(b0)])
        cp(tvs[:], banks[b0][0:16, 0:256], [PB(b0)], ['tvs'])
        dma(tvd[:, :], tvs[:], ['tvs'], ['tvd'])
        for kap in range(64):
            for j_ in range(3):
                src = bass.AP(tensor=tvd, offset=63 - kap + 64 * j_, ap=[[0, 1], [256, 16], [1, 64]])
                dma(biasT[kap:kap + 1, j_, :, :].rearrange("p k (h q) -> p (k h) q", q=64), src, ['tvd'], ['biasT'])

        def load_layer_params(l):
            P = ('prm', l)
            for hf in range(2):
                dma(prm[hf * 64:(hf + 1) * 64, l, PQG:PQG + 1], q_g_d[l].rearrange("(p o) -> p o", o=1), [], [P], slow=True)
                dma(prm[hf * 64:(hf + 1) * 64, l, PKG:PKG + 1], k_g_d[l].rearrange("(p o) -> p o", o=1), [], [P], slow=True)
            ts(prm[:, l, PQG:PQG + 1], prm[:, l, PQG:PQG + 1], 0.125, None, ALU.mult, ALU.bypass, [P], [P])
            dma(prm[:, l, POG:POG + 16], out_g_d[l].rearrange("(ft p) -> p ft", p=128), [], [P], slow=True)
            dma(prm[:, l, PSD:PSD + 4], ssm_d_d[l].rearrange("(ft p) -> p ft", p=128), [], [P], slow=True)
            dma(prm[:, l, PCW:PCW + 12].rearrange("p (ft k) -> p ft k", k=3), conv_w_d[l].rearrange("(ft p) k -> p ft k", p=128),
                [], [P], slow=True)
            dma(es16[:], sinks_d[l:l + 1, :].partition_broadcast(128), [P], ['es16'])

        def sincos(out_sin, out_cos, phi, W, rk, wk):
            for which, dst in ((0, out_sin), (1, out_cos)):
                y = tang[:, 0:W]
                ts(y, phi, 1.0 / TWO_PI, 0.5 + 0.25 * which, ALU.mult, ALU.add, rk + TANG, TANG)
                cp(tangi[:, 0:W], y, TANG + ['xstage'], ['xstage'])
                cp(tang2[:, 0:W], tangi[:, 0:W], ['xstage'] + TANG2, TANG2)
                tt(y, y, tang2[:, 0:W], ALU.subtract, TANG + TANG2, TANG)
                ts(tang2[:, 0:W], y, 0.0, None, ALU.is_lt, ALU.bypass, TANG + TANG2, TANG2)
                tt(y, y, tang2[:, 0:W], ALU.add, TANG + TANG2, TANG)
                ts(y, y, 1.0, -0.5, ALU.min, ALU.add, TANG, TANG)
                act(dst, y, AF.Sin, TANG, wk, scale=TWO_PI)

        def load_ssm(l):
            K = ('sst', l)
            dma(sst[:, 0, :], a_re_d[l].rearrange("(s q) -> q s", q=128), [], [K], slow=True)
            dma(sst[:, 1, :], a_im_d[l].rearrange("(s q) -> q s", q=128), [], [K], slow=True)
            for gh in range(2):
                src = log_dt_d[l].rearrange("(s g) -> g s", g=2)[gh:gh + 1, :].partition_broadcast(64)
                dma(sst[gh * 64:(gh + 1) * 64, 2, :], src, [], [K], slow=True)
            act(sst[:, 2, :], sst[:, 2, :], AF.Exp, [K], [K])
            tt(sst[:, 9, :], sst[:, 2, :], sst[:, 0, :], ALU.mult, [K], [K])
            act(sst[:, 3, :], sst[:, 9, :], AF.Exp, [K], [K])
            tt(sst[:, 4, :], sst[:, 2, :], sst[:, 1, :], ALU.mult, [K], [K])
            sincos(sst[:, 6, :], sst[:, 5, :], sst[:, 4, :], 16, [K], [K])
            tt(sst[:, 10, :], sst[:, 3, :], sst[:, 5, :], ALU.mult, [K], [K])
            tt(sst[:, 11, :], sst[:, 3, :], sst[:, 6, :], ALU.mult, [K], [K])
            ts(sst[:, 12, :], sst[:, 10, :], -1.0, None, ALU.add, ALU.bypass, [K], [K])
            tt(sst[:, 13, :], sst[:, 0, :], sst[:, 0, :], ALU.mult, [K], [K])
            tt(sst[:, 14, :], sst[:, 1, :], sst[:, 1, :], ALU.mult, [K], [K])
            tt(sst[:, 13, :], sst[:, 13, :], sst[:, 14, :], ALU.add, [K], [K])
            rcp(sst[:, 13, :], sst[:, 13, :], [K], [K])
            tt(sst[:, 14, :], sst[:, 12, :], sst[:, 0, :], ALU.mult, [K], [K])
            tt(sst[:, 15, :], sst[:, 11, :], sst[:, 1, :], ALU.mult, [K], [K])
            tt(sst[:, 14, :], sst[:, 14, :], sst[:, 15, :], ALU.add, [K], [K])
            tt(sst[:, 7, :], sst[:, 14, :], sst[:, 13, :], ALU.mult, [K], [K])
            tt(sst[:, 14, :], sst[:, 11, :], sst[:, 0, :], ALU.mult, [K], [K])
            tt(sst[:, 15, :], sst[:, 12, :], sst[:, 1, :], ALU.mult, [K], [K])
            tt(sst[:, 14, :], sst[:, 14, :], sst[:, 15, :], ALU.subtract, [K], [K])
            tt(sst[:, 8, :], sst[:, 14, :], sst[:, 13, :], ALU.mult, [K], [K])
            ang = atmp[:, 0:2, :].rearrange("p a b -> p (a b)")
            tt(ang.rearrange("p (s t) -> p s t", t=LC), sst[:, 4, :].unsqueeze(2).to_broadcast([128, 16, LC]),
               iota[:].unsqueeze(1).to_broadcast([128, 16, LC]), ALU.mult, [K, 'iota', 'xstage'], ['xstage'])
            TE = ('tab', l)
            sincos(tabE[:, 1].rearrange("p s t -> p (s t)"), tabE[:, 0].rearrange("p s t -> p (s t)"), ang, 16 * LC,
                   ['xstage'], [TE])
            frb = sst[:, 7, :].unsqueeze(2).to_broadcast([128, 16, LC])
            fib = sst[:, 8, :].unsqueeze(2).to_broadcast([128, 16, LC])
            t0 = atmp[:, 0:2, :].rearrange("p a (s t) -> p (a s) t", t=LC)
            t1 = atmp[:, 2:4, :].rearrange("p a (s t) -> p (a s) t", t=LC)
            tt(t0, tabE[:, 0], frb, ALU.mult, [TE, K, 'xstage'], ['xstage'])
            tt(t1, tabE[:, 1], fib, ALU.mult, [TE, K, 'xstage'], ['xstage'])
            tt(tabW[:, 0], t0, t1, ALU.add, ['xstage'], [TE])
            tt(t0, tabE[:, 0], fib, ALU.mult, [TE, K, 'xstage'], ['xstage'])
            tt(t1, tabE[:, 1], frb, ALU.mult, [TE, K, 'xstage'], ['xstage'])
            tt(tabW[:, 1], t0, t1, ALU.subtract, ['xstage'], [TE])
            for m in range(3):
                n = 4096.0 * (m + 1)
                act(a4k[:, 2 * m, :], sst[:, 9, :], AF.Exp, [K], [('a4k', l)], scale=n)
                ts(sst[:, 16, :], sst[:, 4, :], n, None, ALU.mult, ALU.bypass, [K], [K])
                sincos(sst[:, 17, :], sst[:, 18, :], sst[:, 16, :], 16, [K], [K])
                tt(a4k[:, 2 * m + 1, :], a4k[:, 2 * m, :], sst[:, 17, :], ALU.mult, [K, ('a4k', l)], [('a4k', l)])
                tt(a4k[:, 2 * m, :], a4k[:, 2 * m, :], sst[:, 18, :], ALU.mult, [K, ('a4k', l)], [('a4k', l)])
            for (wt, dre, dim_, isB) in ((fBw, b_re_d, b_im_d, True), (Cw, c_re_d, c_im_d, False)):
                WK = ('bcw', l, isB)
                for s in range(16):
                    mset(bcst[:], 0.0, [WK, 'bcst'], ['bcst'], eng='pool')
                    for c, dd in ((0, dre), (1, dim_)):
                        for gh in range(2):
                            g = 2 * s + gh
                            r0 = 32 * (s % 4) + 16 * gh
                            if isB:
                                dma(bcst[r0:r0 + 16, c, 64 * gh:64 * gh + 64], dd[l, g].rearrange("p h -> h p"),
                                    ['bcst'], ['bcst'], eng='pool', slow=True)
                            else:
                                dma(bcst[64 * gh:64 * gh + 64, c, r0:r0 + 16], dd[l, g].rearrange("h p -> p h"),
                                    ['bcst'], ['bcst'], eng='pool', slow=True)
                    cp(wt[:, s, :, :], bcst[:], ['bcst'], [WK], eng='pool')
            dma(wglu[:], w_glu_d[l].rearrange("(kt p) c -> p kt c", p=128), [], [('wglu', l)], eng='pool')

        def compute_mod(l):
            MK = ('mod', l)
            for ci in range(48):
                wv, wk = w_get('ada', l, ci)
                for f in range(2):
                    ftm = 2 * ci + f
                    b = bank('aux')
                    for kt in range(16):
                        mm(banks[b][:, 0:5], wv[:, kt, f * 128:(f + 1) * 128], scT[:, kt, :], kt == 0, kt == 15,
                           [wk, 'scT'], [PB(b)])
                    ts(modT[:, l, ftm, :], banks[b][:, 0:5], badaT[:, l, ftm:ftm + 1], None, ALU.add, ALU.bypass,
                       [PB(b), ('bada', l)], [MK])
            ts(sc1p[:, l, 0], modT[:, l, 16:32, :], 1.0, None, ALU.add, ALU.bypass, [MK], [MK])
            ts(sc1p[:, l, 1], modT[:, l, 64:80, :], 1.0, None, ALU.add, ALU.bypass, [MK], [MK])

        def tile_info(t):
            if t < NPT:
                return dict(t=t, segs=[(0, NT, 0, 0)], prompt=True, first=(t == 0), last=(t == NPT - 1))
            return dict(t=t, segs=[(i * 64, 64, 1 + i, 1 + i) for i in range(4)], prompt=False, first=True, last=True)

        XK = [('xT', ft) for ft in range(16)]
        HK = [('hb', kt) for kt in range(16)]

        def load_x0(ti):
            t = ti['t']
            src = xp if ti['prompt'] else xs
            r0 = t * NT if ti['prompt'] else 0
            for blk in range(NT // 128):
                dma(xstage[:], src[r0 + blk * 128:r0 + (blk + 1) * 128, :], [], ['xstage'], eng='sp')
                for f4 in range(4):
                    b = bank('aux')
                    for j in range(4):
                        ft = f4 * 4 + j
                        tr(banks[b][:, j * 128:(j + 1) * 128], xstage[:, ft * 128:(ft + 1) * 128], ident[:],
                           ['xstage', 'ident'], [PB(b)])
                    S.add('act', (lambda e, o=xT[:, f4 * 4:f4 * 4 + 4, blk * 128:(blk + 1) * 128],
                                  i=banks[b][:, :].rearrange("p (j c) -> p j c", j=4): e.activation(out=o, in_=i, func=AF.Copy)),
                          [PB(b)], [('xT', f4 * 4 + j) for j in range(4)])

        def load_x1(ti):
            dma(xT[:].rearrange("p a b -> p (a b)"), x1s[ti['t']], [('x1s', ti['t'])], XK, eng='sp')

        def store_x1(ti):
            dma(x1s[ti['t']], xT[:].rearrange("p a b -> p (a b)"), XK, [('x1s', ti['t'])], eng='sp')

        def store_y(ti):
            t = ti['t']
            dst = yp if ti['prompt'] else ys
            r0 = t * NT if ti['prompt'] else 0
            for blk in range(NT // 128):
                for f4 in range(4):
                    b = bank('aux')
                    for j in range(4):
                        ft = f4 * 4 + j
                        tr(banks[b][:, j * 128:(j + 1) * 128], xT[:, ft, blk * 128:(blk + 1) * 128], ident[:],
                           [('xT', ft), 'ident'], [PB(b)])
                    S.add('act', (lambda e, o=xstage[:, f4 * 512:(f4 + 1) * 512], i=banks[b][:, :]:
                                  e.activation(out=o, in_=i, func=AF.Copy)), [PB(b)], ['xstage'])
                dma(dst[r0 + blk * 128:r0 + (blk + 1) * 128, :], xstage[:], ['xstage'], ['yout'], eng='sp')

        def norm_mod(l, ti, which):
            MK = ('mod', l)
            b = bank('aux')
            for ft in range(16):
                act(sq[:, ft % 2, :], xT[:, ft, :], AF.Square, [('xT', ft)], [('sq', ft % 2)])
                mm(banks[b][:, 0:NT], onesb[:, :], sq[:, ft % 2, :], ft == 0, ft == 15, ['onesb', ('sq', ft % 2)], [PB(b)])
            act(rstd[:, 0, :], banks[b][:, 0:NT], AF.Sqrt, [PB(b)], ['rstd0'], scale=1.0 / 2048.0, bias=EPS)
            rcp(rstd[:, 0, :], rstd[:, 0, :], ['rstd0'], ['rstd0'])
            shift_base = 0 if which == 0 else 48
            for ft in range(16):
                for (c0, ln, r, sid) in ti['segs']:
                    tm = tmpf[:, ft % 2, c0:c0 + ln]
                    stt(tm, xT[:, ft, c0:c0 + ln], sc1p[:, l, which, ft, r:r + 1], rstd[:, 0, c0:c0 + ln], ALU.mult, ALU.mult,
                        [('xT', ft), MK, 'rstd0'], [('tmpf', ft % 2)])
                    act(hb[:, ft, c0:c0 + ln], tm, AF.Identity, [('tmpf', ft % 2), MK], [('hb', ft)],
                        bias=modT[:, l, shift_base + ft, r:r + 1])

        def head_norm(psb, gidx, l, out_bf, out_f32, rk, wk):
            act(sq[:, 0, :], banks[psb][:, 0:NT], AF.Square, [PB(psb)], [('sq', 0)])
            b2 = bank('aux')
            mm(banks[b2][:, 0:NT], ones64[:, :], sq[:, 0, :], True, True, ['ones64', ('sq', 0)], [PB(b2)])
            act(rstd[:, 1, :], banks[b2][:, 0:NT], AF.Sqrt, [PB(b2)], ['rstd1'], scale=1.0 / 64.0, bias=EPS)
            rcp(rstd[:, 1, :], rstd[:, 1, :], ['rstd1'], ['rstd1'])
            if out_f32 is not None:
                stt(out_f32, banks[psb][:, 0:NT], prm[:, l, gidx:gidx + 1], rstd[:, 1, :], ALU.mult, ALU.mult,
                    [PB(psb), 'rstd1', ('prm', l)], wk)
                cp(out_bf, out_f32, wk, rk)
            else:
                stt(out_bf, banks[psb][:, 0:NT], prm[:, l, gidx:gidx + 1], rstd[:, 1, :], ALU.mult, ALU.mult,
                    [PB(psb), 'rstd1', ('prm', l)], rk)

        def seg_zoff(ti, si):
            return si * 66 if not ti['prompt'] else 0

        def in_proj(l, ti, parts):
            order = []
            if 'u' in parts:
                order += [0, 1]
            if 'q' in parts:
                order += [2, 3, 4, 5]
            if 'kv' in parts:
                order += [6]
            if 'conv' in parts:
                order += [7, 8, 11, 12, 9, 10]
            for ci in order:
                wv, wk = w_get('in', l, ci)
                for f in range(2):
                    ft = 2 * ci + f
                    if ft == 13:
                        for ch in range(NT // 64):
                            b = bank('acc')
                            for kt in range(16):
                                mm(banks[b][0:64, 0:128], hb[:, kt, ch * 64:(ch + 1) * 64], wv[:, kt, 128:256], kt == 0, kt == 15,
                                   [('hb', kt), wk], [PB(b)])
                            cp(vf[:, ch, :], banks[b][0:64, 0:128], [PB(b)], [('vf', ch)])
                            for (c0, ln, r, sid) in ti['segs']:
                                if c0 <= ch * 64 < c0 + ln:
                                    si = ti['segs'].index((c0, ln, r, sid))
                                    kc = (si * 3 if not ti['prompt'] else 0) + 2 + (ch * 64 - c0) // 64
                            for dup in range(2):
                                act(vtok[:, kc, :, dup * 64:(dup + 1) * 64], vf[:, ch, :].rearrange("p (k d) -> p k d", k=2), AF.Copy,
                                    [('vf', ch)], [('vtok', kc)])
                        continue
                    b = bank('acc')
                    for kt in range(16):
                        mm(banks[b][:, 0:NT], wv[:, kt, f * 128:(f + 1) * 128], hb[:, kt, :], kt == 0, kt == 15,
                           [('hb', kt), wk], [PB(b)])
                    if ft < 4:
                        act(uT[:, ft, :], banks[b][:, 0:NT], AF.Copy, [PB(b)], [('uT', ft)])
                        cp(uTb[:, ft, :], banks[b][:, 0:NT], [PB(b)], [('uTb', ft)])
                    elif ft < 12:
                        head_norm(b, PQG, l, qT[:, ft - 4, :], None, [('qT', ft - 4)], None)
                    elif ft == 12:
                        head_norm(b, PKG, l, knb[:, :], knf[:, :], ['knb'], ['knf'])
                        for kv in range(2):
                            b2 = bank('aux')
                            mm(banks[b2][:, 0:NT], selk[:, kv, :], knb[:, :], True, True, ['selk', 'knb'], [PB(b2)])
                            for si, (c0, ln, r, sid) in enumerate(ti['segs']):
                                kc0 = (si * 3 if not ti['prompt'] else 0) + 2
                                act(kdup[:, kv, kc0 * 64:kc0 * 64 + ln], banks[b2][:, c0:c0 + ln], AF.Copy, [PB(b2)],
                                    [('kdup', si)])
                    elif ft < 18:
                        act(gbT[:, ft - 14, :], banks[b][:, 0:NT], AF.Copy, [PB(b)], [('gbT', ft - 14)])
                    elif ft >= 22:
                        j = ft - 22
                        for si, (c0, ln, r, sid) in enumerate(ti['segs']):
                            zo = seg_zoff(ti, si) + 2
                            act(zT[:, j, zo:zo + ln], banks[b][:, c0:c0 + ln], AF.Copy, [PB(b)], [('zT', j)])
                    else:
                        j = ft - 18
                        for si, (c0, ln, r, sid) in enumerate(ti['segs']):
                            zo = seg_zoff(ti, si) + 2
                            tt(zT[:, j, zo:zo + ln], banks[b][:, c0:c0 + ln], zT[:, j, zo:zo + ln], ALU.mult,
                               [PB(b), ('zT', j)], [('zT', j)])

        def ssm(l, ti, full, carry_of):
            K = ('sst', l)
            TE = ('tab', l)
            T = [('ssmt', i) for i in range(8)]
            for s in range(16):
                ftu = s // 4
                bre, bim = bank('att'), bank('att')
                mm(banks[bre][:, 0:NT], fBw[:, s, 0, :], uTb[:, ftu, :], True, True, [('bcw', l, True), ('uTb', ftu)], [PB(bre)])
                mm(banks[bim][:, 0:NT], fBw[:, s, 1, :], uTb[:, ftu, :], True, True, [('bcw', l, True), ('uTb', ftu)], [PB(bim)])
                nsub = NT // LC

                def v3(ap):
                    return ap.rearrange("p (a b) -> p a b", b=LC)
                Wr = tabW[:, 0, s, :].unsqueeze(1).to_broadcast([128, nsub, LC])
                Wi = tabW[:, 1, s, :].unsqueeze(1).to_broadcast([128, nsub, LC])
                tt(v3(ssmt[:, 0, :]), v3(banks[bre][:, 0:NT]), Wr, ALU.mult, [PB(bre), TE], [T[0]])
                tt(v3(ssmt[:, 1, :]), v3(banks[bim][:, 0:NT]), Wi, ALU.mult, [PB(bim), TE], [T[1]])
                tt(ssmt[:, 2, :], ssmt[:, 0, :], ssmt[:, 1, :], ALU.subtract, [T[0], T[1]], [T[2]])
                tt(v3(ssmt[:, 0, :]), v3(banks[bre][:, 0:NT]), Wi, ALU.mult, [PB(bre), TE, T[0]], [T[0]])
                tt(v3(ssmt[:, 1, :]), v3(banks[bim][:, 0:NT]), Wr, ALU.mult, [PB(bim), TE, T[1]], [T[1]])
                tt(ssmt[:, 3, :], ssmt[:, 0, :], ssmt[:, 1, :], ALU.add, [T[0], T[1]], [T[3]])
                for (c0, ln, r, sid) in ti['segs']:
                    cid = carry_of(sid)
                    CK = ('car', cid, s)
                    for sc in range(ln // LC):
                        a, bnd = c0 + sc * LC, c0 + (sc + 1) * LC
                        magb = sst[:, 3, s:s + 1].to_broadcast([128, LC])
                        scan(ssmt[:, 4, a:bnd], magb, ssmt[:, 2, a:bnd], car[:, cid, 0, s:s + 1], [K, T[2], CK], [T[4]])
                        scan(ssmt[:, 5, a:bnd], magb, ssmt[:, 3, a:bnd], car[:, cid, 1, s:s + 1], [K, T[3], CK], [T[5]])
                        lo, hi = LC - 1, LC
                        Cc = tabE[:, 0, s, lo:hi]
                        Sc = tabE[:, 1, s, lo:hi]
                        gr = ssmt[:, 4, a + lo:a + hi]
                        gi = ssmt[:, 5, a + lo:a + hi]
                        c0_, c1_, c2_, c3_ = (fence[:, 0:1], fence[:, 1:2], fence[:, 2:3], fence[:, 3:4])
                        FK = 'fence'
                        tt(c0_, gr, Cc, ALU.mult, [T[4], TE, FK], [FK])
                        tt(c1_, gi, Sc, ALU.mult, [T[5], TE, FK], [FK])
                        tt(car[:, cid, 0, s:s + 1], c0_, c1_, ALU.subtract, [FK, CK], [CK])
                        tt(c2_, gr, Sc, ALU.mult, [T[4], TE, FK], [FK])
                        tt(c3_, gi, Cc, ALU.mult, [T[5], TE, FK], [FK])
                        tt(car[:, cid, 1, s:s + 1], c2_, c3_, ALU.add, [FK, CK], [CK])
                if full:
                    Cb = tabE[:, 0, s, :].unsqueeze(1).to_broadcast([128, nsub, LC])
                    Sb = tabE[:, 1, s, :].unsqueeze(1).to_broadcast([128, nsub, LC])
                    P0, P1_ = ('tmpf', 0), ('tmpf', 1)
                    tt(v3(tmpf[:, 0, :]), v3(ssmt[:, 4, :]), Cb, ALU.mult, [T[4], TE, P0], [P0], eng='pool')
                    tt(v3(tmpf[:, 1, :]), v3(ssmt[:, 5, :]), Sb, ALU.mult, [T[5], TE, P1_], [P1_], eng='pool')
                    tt(ssmt[:, 6, :], tmpf[:, 0, :], tmpf[:, 1, :], ALU.subtract, [P0, P1_], [T[6]], eng='pool')
                    tt(v3(tmpf[:, 0, :]), v3(ssmt[:, 4, :]), Sb, ALU.mult, [T[4], TE, P0], [P0], eng='pool')
                    tt(v3(tmpf[:, 1, :]), v3(ssmt[:, 5, :]), Cb, ALU.mult, [T[5], TE, P1_], [P1_], eng='pool')
                    tt(ssmt[:, 7, :], tmpf[:, 0, :], tmpf[:, 1, :], ALU.add, [P0, P1_], [T[7]], eng='pool')
                if full:
                    act(hrb[:, 0, :], ssmt[:, 6, :], AF.Copy, [T[6]], [('hrb', s % 2, 0)])
                    act(hrb[:, 1, :], ssmt[:, 7, :], AF.Copy, [T[7]], [('hrb', s % 2, 1)], scale=-1.0)
                    if s % 4 == 0:
                        ybank = bank('acc')
                        ssm.yb = ybank
                    yb = ssm.yb
                    mm(banks[yb][:, 0:NT], Cw[:, s, 0, :], hrb[:, 0, :], s % 4 == 0, False, [('bcw', l, False), ('hrb', s % 2, 0)], [PB(yb)])
                    mm(banks[yb][:, 0:NT], Cw[:, s, 1, :], hrb[:, 1, :], False, s % 4 == 3, [('bcw', l, False), ('hrb', s % 2, 1)], [PB(yb)])
                    if s % 4 == 3:
                        ft = s // 4
                        stt(ygrp[:, ft, :], uT[:, ft, :], prm[:, l, PSD + ft:PSD + ft + 1], banks[yb][:, 0:NT], ALU.mult, ALU.add,
                            [('uT', ft), ('prm', l), PB(yb)], [('yg', ft)])
                        act(tmpf[:, 2, :], ygrp[:, ft, :], AF.Square, [('yg', ft)], [('tmpf', 2)])
                        ts(tmpf[:, 2, :], tmpf[:, 2, :], 0.044715, 1.0, ALU.mult, ALU.add, [('tmpf', 2)], [('tmpf', 2)])
                        tt(tmpf[:, 2, :], tmpf[:, 2, :], ygrp[:, ft, :], ALU.mult, [('tmpf', 2), ('yg', ft)], [('tmpf', 2)])
                        act(tmpf[:, 2, :], tmpf[:, 2, :], AF.Sigmoid, [('tmpf', 2)], [('tmpf', 2)], scale=2.0 * math.sqrt(2.0 / math.pi))
                        tt(ygrp[:, ft, :], ygrp[:, ft, :], tmpf[:, 2, :], ALU.mult, [('tmpf', 2), ('yg', ft)], [('yg', ft)])
                        cp(yT[:, ft, :], ygrp[:, ft, :], [('yg', ft)], [('yT', ft)])
            if not full:
                return
            for fo in range(4):
                b = bank('acc')
                for kt in range(4):
                    mm(banks[b][:, 0:NT], wglu[:, kt, fo * 128:(fo + 1) * 128], yT[:, kt, :], kt == 0, kt == 3,
                       [('wglu', l), ('yT', kt)], [PB(b)])
                act(tmpf[:, 2, :], banks[b][:, 0:NT], AF.Sigmoid, [PB(b)], [('tmpf', 2)])
                tt(ygrp[:, 4 + fo, :], ygrp[:, fo, :], tmpf[:, 2, :], ALU.mult, [('tmpf', 2), ('yg', fo)], [('yg', 4 + fo)])
            group_norm_write(l, [4, 5, 6, 7], [0, 1, 2, 3], 512)

        def group_norm_write(l, gsrc, ydst, width):
            b = bank('aux')
            n = len(gsrc)
            for i, g in enumerate(gsrc):
                act(sq[:, i % 2, :], ygrp[:, g, :], AF.Square, [('yg', g)], [('sq', i % 2)])
                mm(banks[b][:, 0:NT], onesb[:, :], sq[:, i % 2, :], i == 0, i == n - 1, ['onesb', ('sq', i % 2)], [PB(b)])
            act(rstd[:, 2, :], banks[b][:, 0:NT], AF.Sqrt, [PB(b)], ['rstd2'], scale=1.0 / width, bias=EPS)
            rcp(rstd[:, 2, :], rstd[:, 2, :], ['rstd2'], ['rstd2'])
            for g, yd in zip(gsrc, ydst):
                stt(yT[:, yd, :], ygrp[:, g, :], prm[:, l, POG + yd:POG + yd + 1], rstd[:, 2, :], ALU.mult, ALU.mult,
                    [('yg', g), ('prm', l), 'rstd2'], [('yT', yd)])

        def conv(l, ti):
            for j in range(4):
                for si, (c0, ln, r, sid) in enumerate(ti['segs']):
                    zo = seg_zoff(ti, si)
                    o = ygrp[:, j, c0:c0 + ln]
                    rk = [('zT', j), ('prm', l)]
                    ts(o, zT[:, j, zo:zo + ln], prm[:, l, PCW + 3 * j:PCW + 3 * j + 1], None, ALU.mult, ALU.bypass, rk, [('yg', j)])
                    stt(o, zT[:, j, zo + 1:zo + 1 + ln], prm[:, l, PCW + 3 * j + 1:PCW + 3 * j + 2], o, ALU.mult, ALU.add,
                        rk + [('yg', j)], [('yg', j)])
                    stt(o, zT[:, j, zo + 2:zo + 2 + ln], prm[:, l, PCW + 3 * j + 2:PCW + 3 * j + 3], o, ALU.mult, ALU.add,
                        rk + [('yg', j)], [('yg', j)])
                tt(ygrp[:, j, :], ygrp[:, j, :], gbT[:, j, :], ALU.mult, [('yg', j), ('gbT', j)], [('yg', j)])
            group_norm_write(l, [0, 1, 2, 3], [12, 13, 14, 15], 512)

        def conv_carry(ti):
            for j in range(4):
                cp(zT[:, j, 0:2], zT[:, j, NT:NT + 2], [('zT', j)], [('zT', j)])

        def attention(l, ti):
            for si, (c0, ln, r, sid) in enumerate(ti['segs']):
                cb = si * 3 if not ti['prompt'] else 0
                nq = ln // 64
                for qc in range(nq):
                    q0 = c0 + qc * 64
                    for kv in range(2):
                        for j in range(3):
                            kc = cb + 2 + qc - j
                            for a in range(2):
                                b = bank('att')
                                mm(banks[b][0:64, 0:256].rearrange("p (t q) -> p t q", q=64),
                                   kdup[a * 64:(a + 1) * 64, kv, kc * 64:(kc + 1) * 64],
                                   qT[a * 64:(a + 1) * 64, 4 * kv:4 * kv + 4, q0:q0 + 64], True, True,
                                   [('kdup', si), ('qT', 4 * kv), ('qT', 4 * kv + 1), ('qT', 4 * kv + 2), ('qT', 4 * kv + 3)], [PB(b)])
                                tt(tS[:, j % 2, a * 256:(a + 1) * 256], banks[b][0:64, 0:256], biasT[:, j, kv, a * 256:(a + 1) * 256], ALU.add,
                                   [PB(b), 'biasT'], [('tS', j % 2)])
                            halo_first = ti['prompt'] and ti['first'] and (qc - j) < 0
                            if halo_first:
                                act(PT[:, kv, j, :], tS[:, j % 2, :], AF.Exp, [('tS', j % 2), 'flg'], [('PT', kv, j)],
                                    bias=flg[0:64, 12:13])
                            else:
                                act(PT[:, kv, j, :], tS[:, j % 2, :], AF.Exp, [('tS', j % 2)], [('PT', kv, j)])
                        bpv, bden = bank('att'), bank('aux')
                        for j in range(3):
                            kc = cb + 2 + qc - j
                            mm(banks[bpv][:, :], vtok[:, kc, kv, :], PT[:, kv, j, :], j == 0, j == 2, [('vtok', kc), ('PT', kv, j)], [PB(bpv)])
                        for j in range(3):
                            mm(banks[bden][:, :], onesb[0:64, :], PT[:, kv, j, :], j == 0, j == 2, ['onesb', ('PT', kv, j)], [PB(bden)])
                        tt(atmp[:, 0, :].rearrange("p (h q) -> p h q", q=64), banks[bden][:, :].rearrange("p (h q) -> p h q", q=64),
                           es16[:, kv * 8:(kv + 1) * 8].unsqueeze(2).to_broadcast([128, 8, 64]), ALU.add, [PB(bden), 'es16'], ['xstage'])
                        rcp(atmp[:, 0, :], atmp[:, 0, :], ['xstage'], ['xstage'])
                        for a in range(2):
                            pr = slice(a * 64, (a + 1) * 64)
                            tt(ygrp[pr, 4 * kv:4 * kv + 4, q0:q0 + 64],
                               banks[bpv][pr, a * 256:(a + 1) * 256].rearrange("p (t q) -> p t q", q=64),
                               atmp[pr, 0, a * 256:(a + 1) * 256].rearrange("p (t q) -> p t q", q=64), ALU.mult,
                               [PB(bpv), 'xstage'], [('yg', 4 * kv + t) for t in range(4)])
            group_norm_write(l, list(range(8)), list(range(4, 12)), 1024)

        def kv_carry(ti):
            for kv in range(2):
                cp(kdup[:, kv, 0:128], kdup[:, kv, (2 + NT // 64 - 2) * 64:(2 + NT // 64) * 64], [('kdup', 0)], [('kdup', 0)])
            n = NT // 64
            for c in range(2):
                cp(vtok[:, c].rearrange("p a b -> p (a b)"), vtok[:, n + c].rearrange("p a b -> p (a b)"),
                   [('vtok', n + c)], [('vtok', c)])

        def out_proj(l, ti):
            MK = ('mod', l)
            for ci in range(8):
                wv, wk = w_get('out', l, ci)
                for f in range(2):
                    ft = 2 * ci + f
                    b = bank('acc')
                    for kt in range(16):
                        mm(banks[b][:, 0:NT], wv[:, kt, f * 128:(f + 1) * 128], yT[:, kt, :], kt == 0, kt == 15,
                           [wk, ('yT', kt)], [PB(b)])
                    for (c0, ln, r, sid) in ti['segs']:
                        stt(xT[:, ft, c0:c0 + ln], banks[b][:, c0:c0 + ln], modT[:, l, 32 + ft, r:r + 1], xT[:, ft, c0:c0 + ln],
                            ALU.mult, ALU.add, [PB(b), MK, ('xT', ft)], [('xT', ft)])

        def ffn(l, ti):
            MK = ('mod', l)
            for gi in range(4):
                for cc in range(8):
                    wv, wk = w_get('ff1', l, gi * 8 + cc)
                    for f in range(2):
                        hf = cc * 2 + f
                        b = bank('acc')
                        for kt in range(16):
                            mm(banks[b][:, 0:NT], wv[:, kt, f * 128:(f + 1) * 128], hb[:, kt, :], kt == 0, kt == 15,
                               [wk, ('hb', kt)], [PB(b)])
                        act(tmpf[:, 3, :], banks[b][:, 0:NT], AF.Relu, [PB(b)], [('tmpf', 3)])
                        tt(hid[:, hf, :], tmpf[:, 3, :], tmpf[:, 3, :], ALU.mult, [('tmpf', 3)], [('hid', hf)])
                for oc in range(8):
                    wv, wk = w_get('ff2', l, gi * 8 + oc)
                    for f in range(2):
                        ft = 2 * oc + f
                        b = bank('acc')
                        for kt in range(16):
                            mm(banks[b][:, 0:NT], wv[:, kt, f * 128:(f + 1) * 128], hid[:, kt, :], kt == 0, kt == 15,
                               [wk, ('hid', kt)], [PB(b)])
                        for (c0, ln, r, sid) in ti['segs']:
                            stt(xT[:, ft, c0:c0 + ln], banks[b][:, c0:c0 + ln], modT[:, l, 80 + ft, r:r + 1], xT[:, ft, c0:c0 + ln],
                                ALU.mult, ALU.add, [PB(b), MK, ('xT', ft)], [('xT', ft)])

        def install_halo(ti, si, kn_ap, v_ap, z_ap, rk):
            cb = si * 3 if not ti['prompt'] else 0
            cp(knb[:, 0:128], kn_ap, rk + ['knb'], ['knb'])
            for kv in range(2):
                b2 = bank('aux')
                mm(banks[b2][:, 0:128], selk[:, kv, :], knb[:, 0:128], True, True, ['selk', 'knb'], [PB(b2)])
                act(kdup[:, kv, cb * 64:cb * 64 + 128], banks[b2][:, 0:128], AF.Copy, [PB(b2)], [('kdup', si)])
            for c in range(2):
                for dup in range(2):
                    act(vtok[:, cb + c, :, dup * 64:(dup + 1) * 64], v_ap[:, c, :].rearrange("p (k d) -> p k d", k=2), AF.Copy,
                        rk, [('vtok', cb + c)])
            zo = seg_zoff(ti, si)
            for j in range(4):
                cp(zT[:, j, zo:zo + 2], z_ap[:, j, :], rk + [('zT', j)], [('zT', j)])

        def load_sample_state(l, ti):
            for si in range(4):
                dma(hal[:, 32:160], ck[l, si].rearrange("t f -> f t"), ['hal'], ['hal'], slow=True)
                dma(hal[0:64, 160:416].rearrange("p (c f) -> p c f", c=2), cv[l, si].rearrange("(c k) f -> k c f", c=2), ['hal'], ['hal'])
                for j_ in range(4):
                    dma(hal[:, 416 + 2 * j_:418 + 2 * j_], scv[l, si].rearrange("t (j p) -> p j t", p=128)[:, j_, :],
                        ['hal'], ['hal'], slow=True)
                install_halo(ti, si, hal[:, 32:160], hal[0:64, 160:416].rearrange("p (c f) -> p c f", c=2),
                             hal[:, 416:424].rearrange("p (j t) -> p j t", t=2), ['hal'])
                for s_ in range(16):
                    pass
                dma(car[:, 1 + si, 0, :], sre[l, si].rearrange("(s q) -> q s", q=128), [('car', 1 + si, s) for s in range(16)],
                    [('car', 1 + si, s) for s in range(16)], slow=True)
                dma(car[:, 1 + si, 1, :], sim[l, si].rearrange("(s q) -> q s", q=128), [('car', 1 + si, s) for s in range(16)],
                    [('car', 1 + si, s) for s in range(16)], slow=True)

        def exchange(l, ti_last):
            CK5 = [('car', 5, s) for s in range(16)]
            CK0 = [('car', 0, s) for s in range(16)]
            cp(exb[:, 0:16], car[:, 5, 0, :], CK5, ['exb'])
            cp(exb[:, 16:32], car[:, 5, 1, :], CK5 + ['exb'], ['exb'])
            cp(exb[:, 32:160], knf[:, NT - 128:NT], ['knf', 'exb'], ['exb'])
            mset(exb[64:128, 160:416], 0.0, ['exb'], ['exb'])
            n = NT // 64
            cp(exb[0:64, 160:416].rearrange("p (c f) -> p c f", c=2), vf[:, n - 2:n, :], [('vf', n - 2), ('vf', n - 1), 'exb'], ['exb'])
            cp(exb[:, 416:424].rearrange("p (j t) -> p j t", t=2), zT[:, :, NT:NT + 2], [('zT', j) for j in range(4)] + ['exb'], ['exb'])
            mset(exb[:, 424:EXW], 0.0, ['exb'], ['exb'])
            dma(cc_in[l][:, :], exb[:], ['exb'], [('ccin', l)])
            S.add('pool', lambda e: e.collective_compute("AllGather", ALU.bypass, replica_groups=[[0, 1, 2, 3], [4, 5, 6, 7]],
                                                         ins=[cc_in[l].ap().opt()], outs=[cc_out[l].ap().opt()]),
                  [('ccin', l)], [('ccout', l)])
            dma(gth[:], cc_out[l].ap().rearrange("(r p) c -> p r c", p=128), [('ccout', l), 'xstage'], ['xstage'])
            SK = [('ssmt', i) for i in range(4)]
            for m in range(3):
                o = ssmt[:, m, 0:32]
                ts(o, gth[:, 0, 0:32], flg[:, 4 * m:4 * m + 1], None, ALU.mult, ALU.bypass, ['xstage', 'flg', SK[m]], [SK[m]])
                for r in range(1, 4):
                    stt(o, gth[:, r, 0:32], flg[:, 4 * m + r:4 * m + r + 1], o, ALU.mult, ALU.add, ['xstage', 'flg', SK[m]], [SK[m]])
            AK = ('a4k', l)
            cp(car[:, 0, 0, :], ssmt[:, 0, 0:16], [SK[0]] + CK0, CK0)
            cp(car[:, 0, 1, :], ssmt[:, 0, 16:32], [SK[0]] + CK0, CK0)
            for m in (1, 2):
                xr, xi = ssmt[:, m, 0:16], ssmt[:, m, 16:32]
                ar_, ai_ = a4k[:, 2 * (m - 1), :], a4k[:, 2 * (m - 1) + 1, :]
                t0, t1 = ssmt[:, 3, 0:16], ssmt[:, 3, 16:32]
                tt(t0, xr, ar_, ALU.mult, [SK[m], AK, SK[3]], [SK[3]])
                tt(t1, xi, ai_, ALU.mult, [SK[m], AK, SK[3]], [SK[3]])
                tt(t0, t0, t1, ALU.subtract, [SK[3]], [SK[3]])
                tt(car[:, 0, 0, :], car[:, 0, 0, :], t0, ALU.add, [SK[3]] + CK0, CK0)
                tt(t0, xr, ai_, ALU.mult, [SK[m], AK, SK[3]], [SK[3]])
                tt(t1, xi, ar_, ALU.mult, [SK[m], AK, SK[3]], [SK[3]])
                tt(t0, t0, t1, ALU.add, [SK[3]], [SK[3]])
                tt(car[:, 0, 1, :], car[:, 0, 1, :], t0, ALU.add, [SK[3]] + CK0, CK0)
            o = hal[:, 32:424]
            ts(o, gth[:, 0, 32:424], flg[:, 0:1], None, ALU.mult, ALU.bypass, ['xstage', 'flg', 'hal'], ['hal'])
            for r in range(1, 4):
                stt(o, gth[:, r, 32:424], flg[:, r:r + 1], o, ALU.mult, ALU.add, ['xstage', 'flg', 'hal'], ['hal'])
            pt0 = tile_info(0)
            install_halo(pt0, 0, hal[:, 32:160], hal[0:64, 160:416].rearrange("p (c f) -> p c f", c=2),
                         hal[:, 416:424].rearrange("p (j t) -> p j t", t=2), ['hal'])

        def esink_setup(l):
            act(es16[:], es16[:], AF.Exp, ['es16'], ['es16'])

        def emit_state_outputs(l, ti):
            if ti['prompt']:
                specs = [(0, 0, NT, nkp[l], nvp[l], srp[l], sip[l], cvp[l], None)]
            else:
                specs = [(si, c0, ln, nks[l, si], nvs[l, si], srs[l, si], sis[l, si], cvs[l, si], si) for si, (c0, ln, r, sid) in enumerate(ti['segs'])]
            for (si, c0, ln, dk, dv, dr, di, dc, smp) in specs:
                sid = ti['segs'][si][3]
                CK = [('car', sid, s) for s in range(16)]
                dma(dr.rearrange("(s q) -> q s", q=128), car[:, sid, 0, :], CK, ['sout'], slow=True)
                dma(di.rearrange("(s q) -> q s", q=128), car[:, sid, 1, :], CK, ['sout'], slow=True)
                zo = seg_zoff(ti, si)
                for j_ in range(4):
                    dma(dc.rearrange("t (j p) -> p j t", p=128)[:, j_, :], zT[:, j_, zo + ln:zo + ln + 2], [('zT', j_)], ['sout'], slow=True)
                ntr = 2 if smp is None else 1
                for i in range(ntr):
                    cc0 = c0 + ln - 64 * (ntr - i)
                    b = bank('aux')
                    tr(banks[b][0:64, 0:128], knf[:, cc0:cc0 + 64], ident[:], ['knf', 'ident'], [PB(b)])
                    cp(o64[:, i, :], banks[b][0:64, 0:128], [PB(b)], [('o64', i)])
                    row0 = 64 * i if smp is None else 64
                    dma(dk[row0:row0 + 64, :], o64[:, i, :], [('o64', i)], ['sout'])
                    ch = cc0 // 64
                    dma(dv[row0:row0 + 64, :], vf[:, ch, :], [('vf', ch)], ['sout'])
                if smp is not None:
                    dma(dk[0:64, :], ck[l, si, 64:128, :], [], ['sout'])
                    dma(dv[0:64, :], cv[l, si, 64:128, :], [], ['sout'])

        def p1_chunks(l, last):
            seq = [('in', l, 0), ('in', l, 1)]
            if last:
                seq += [('in', l, c) for c in (6, 7, 8, 11, 12, 9, 10)]
            return seq

        def p2_chunks(l):
            seq = [('in', l, c) for c in (0, 1, 2, 3, 4, 5, 6, 7, 8, 11, 12, 9, 10)]
            seq += [('out', l, c) for c in range(8)]
            for gi in range(4):
                seq += [('ff1', l, gi * 8 + c) for c in range(8)]
                seq += [('ff2', l, gi * 8 + c) for c in range(8)]
            return seq

        def P1(l, ti):
            dbg = (l == 0 and ti['t'] == 0)
            if dbg:
                checkpoint()
            norm_mod(l, ti, 0)
            if dbg:
                checkpoint()
            parts = ['u'] + (['kv', 'conv'] if ti['last'] else [])
            in_proj(l, ti, parts)
            if dbg:
                checkpoint()
            ssm(l, ti, False, lambda sid: 5)

        def P2(l, ti):
            dbg = (l == 0 and ti['t'] == 0)
            norm_mod(l, ti, 0)
            in_proj(l, ti, ['u', 'q', 'kv', 'conv'])
            if dbg:
                checkpoint()
            ssm(l, ti, True, lambda sid: sid)
            if dbg:
                checkpoint()
            conv(l, ti)
            if dbg:
                checkpoint()
            attention(l, ti)
            if dbg:
                checkpoint()
            if ti['last']:
                emit_state_outputs(l, ti)
            elif ti['prompt']:
                kv_carry(ti)
                conv_carry(ti)
            out_proj(l, ti)
            if dbg:
                checkpoint()
            norm_mod(l, ti, 1)
            ffn(l, ti)

        tiles = [tile_info(t) for t in range(NPT + 1)]
        for l in range(2):
            plan.extend([('ada', l, c) for c in range(48)])
        for t in range(NPT):
            plan.extend(p1_chunks(0, t == NPT - 1))
        for t in range(NPT + 1):
            plan.extend(p2_chunks(0))
        for t in range(NPT):
            plan.extend(p1_chunks(1, t == NPT - 1))
        for t in range(NPT + 1):
            plan.extend(p2_chunks(1))

        def zero_p1_carry():
            mset(car[:, 5].rearrange("p a b -> p (a b)"), 0.0, [('car', 5, s) for s in range(16)], [('car', 5, s) for s in range(16)])

        def main_seq():
            cast_all('ada', 0)
            cast_all('ada', 1)
            cast_all('in', 0)
            load_layer_params(0)
            load_ssm(0)
            cast_all('out', 0)
            cast_all('ff1', 0)
            cast_all('ff2', 0)
            cast_all('in', 1)
            cast_all('out', 1)
            cast_all('ff1', 1)
            cast_all('ff2', 1)
            checkpoint()
            compute_mod(0)
            compute_mod(1)
            esink_setup(0)
            checkpoint()
            zero_p1_carry()
            for t in range(NPT):
                load_x0(tiles[t])
                P1(0, tiles[t])
                if t == 0:
                    checkpoint()
            checkpoint()
            exchange(0, tiles[NPT - 1])
            checkpoint()
            for t in range(NPT + 1):
                ti = tiles[t]
                load_x0(ti)
                if not ti['prompt']:
                    load_sample_state(0, ti)
                P2(0, ti)
                store_x1(ti)
                if t == 0:
                    checkpoint()
                if t == NPT - 1:
                    checkpoint()
            checkpoint()
            load_layer_params(1)
            load_ssm(1)
            esink_setup(1)
            zero_p1_carry()
            checkpoint()
            for t in range(NPT):
                load_x1(tiles[t])
                P1(1, tiles[t])
            checkpoint()
            exchange(1, tiles[NPT - 1])
            checkpoint()
            for t in range(NPT + 1):
                ti = tiles[t]
                load_x1(ti)
                if not ti['prompt']:
                    load_sample_state(1, ti)
                P2(1, ti)
                store_y(ti)

        try:
            main_seq()
        except _Stop:
            pass
        for e_ in ('sp', 'pool'):
            op_ = S.add(e_, None, reads=['yout', 'sout'])
            for o_ in S.slot_last:
                if o_ is not None:
                    op_.deps.add(o_)
        S.emit(block, sems, dsem)
    return nc


def _bucket(rel):
    half, exact = 16, 8
    n = np.abs(rel)
    nf = np.maximum(n, 1).astype(np.float32)
    far = exact + (np.log(nf / exact) / math.log(64 / exact) * (half - exact)).astype(np.int32)
    far = np.minimum(far, half - 1)
    return np.where(rel > 0, half, 0) + np.where(n < exact, n, far)


_NC_CACHE = {}


def kernel(**inp):
    f32 = np.float32
    inp = {k: np.ascontiguousarray(np.asarray(v), dtype=f32) for k, v in inp.items()}
    if 'nc' not in _NC_CACHE:
        _NC_CACHE['nc'] = build_program()
    nc = _NC_CACHE['nc']
    e = np.arange(256)
    rel = 63 - e
    bk = _bucket(rel)
    ohe = np.zeros((32, 256), f32)
    ohe[bk, e] = 1.0
    ident = np.eye(128, dtype=f32)
    iota = np.arange(1, LC + 1, dtype=f32)[None, :]
    shared = ['rel_bias', 'w_ada', 'b_ada', 'w_in', 'ssm_log_dt', 'ssm_b_re', 'ssm_b_im', 'ssm_c_re', 'ssm_c_im', 'ssm_d',
              'ssm_w_glu', 'q_norm_g', 'k_norm_g', 'attn_sinks', 'conv_w', 'out_norm_g', 'w_out', 'w_ff1', 'w_ff2']
    maps = []
    for c in range(8):
        b, j = c // 4, c % 4
        m = {k: inp[k] for k in shared}
        m['ssm_a_re'] = inp['ssm_a_re'].reshape(2, 2048)
        m['ssm_a_im'] = inp['ssm_a_im'].reshape(2, 2048)
        m['xp'] = inp['x_prompt'][b, j * 4096:(j + 1) * 4096]
        m['xs'] = inp['x_sample'][4 * c:4 * c + 4].reshape(256, 2048)
        m['ck'] = inp['cache_k'][:, 4 * c:4 * c + 4].reshape(2, 4, 128, 128)
        m['cv'] = inp['cache_v'][:, 4 * c:4 * c + 4].reshape(2, 4, 128, 128)
        m['sre'] = inp['state_ssm_re'][:, 4 * c:4 * c + 4].reshape(2, 4, 2048)
        m['sim'] = inp['state_ssm_im'][:, 4 * c:4 * c + 4].reshape(2, 4, 2048)
        m['scv'] = inp['state_conv'][:, 4 * c:4 * c + 4]
        m['cvec'] = np.concatenate([inp['c_prompt'][b:b + 1], inp['c_sample'][4 * c:4 * c + 4]], 0)
        fl = np.zeros((1, 16), f32)
        for mm_ in range(3):
            r = j - 1 - mm_
            if r >= 0:
                fl[0, 4 * mm_ + r] = 1.0
        fl[0, 12] = 0.0 if j > 0 else -30000.0
        m['flags'] = fl
        m['ident'] = ident
        m['iota'] = iota
        m['ohe'] = ohe
        maps.append({k: np.ascontiguousarray(v) for k, v in m.items()})
    res = run_bass_kernel_spmd(nc, maps, core_ids=list(range(8)))
    R = res.results
    y_p = np.zeros((2, 16384, 2048), f32)
    for c in range(8):
        y_p[c // 4, (c % 4) * 4096:(c % 4 + 1) * 4096] = R[c]['yp']
    y_s = np.concatenate([R[c]['ys'].reshape(4, 64, 2048) for c in range(8)], 0)
    lastc = [3, 7]
    nk_p = np.stack([R[c]['nkp'] for c in lastc], 1).reshape(2, 2, 128, 2, 64)
    nv_p = np.stack([R[c]['nvp'] for c in lastc], 1).reshape(2, 2, 128, 2, 64)
    sr_p = np.stack([R[c]['srp'] for c in lastc], 1).reshape(2, 2, 32, 64)
    si_p = np.stack([R[c]['sip'] for c in lastc], 1).reshape(2, 2, 32, 64)
    cv_p = np.stack([R[c]['cvp'] for c in lastc], 1).reshape(2, 2, 2, 512)
    nk_s = np.concatenate([R[c]['nks'] for c in range(8)], 1).reshape(2, 32, 128, 2, 64)
    nv_s = np.concatenate([R[c]['nvs'] for c in range(8)], 1).reshape(2, 32, 128, 2, 64)
    sr_s = np.concatenate([R[c]['srs'] for c in range(8)], 1).reshape(2, 32, 32, 64)
    si_s = np.concatenate([R[c]['sis'] for c in range(8)], 1).reshape(2, 32, 32, 64)
    cv_s = np.concatenate([R[c]['cvs'] for c in range(8)], 1).reshape(2, 32, 2, 512)
    outs = (y_p, y_s, nk_p, nv_p, sr_p, si_p, cv_p, nk_s, nv_s, sr_s, si_s, cv_s)
    return tuple(np.ascontiguousarray(o, dtype=f32) for o in outs)


def _stats():
    import time
    t0 = time.time()
    nc = build_program()
    print('build', time.time() - t0)
    return nc
```
